# Optimizing a Trainium2 kernel written in Bass

```python
import math
import jax
import jax.numpy as jnp
from jax import lax
import numpy as np

D_MODEL = 2048
BATCH = 4
SEQ = 2048
DEPTH = 2

CHUNK = 128
EPS = 1e-6
D_FF = 5632
N_BRANCH = 4
BRANCH_WIDTH = 1024

A_GROUPS = 8
A_GROUP_DIM = BRANCH_WIDTH // A_GROUPS

B_HEAD_DIM = 64
B_HEADS = BRANCH_WIDTH // B_HEAD_DIM
B_GROUPS = 2
B_STATE = 128
B_CONV = 4
B_CONV_DIM = BRANCH_WIDTH + 2 * B_GROUPS * B_STATE

C_HEADS = 8
C_HEAD_QK = 64
C_QK = C_HEADS * C_HEAD_QK
C_HEAD_V = BRANCH_WIDTH // C_HEADS
ROPE_BASE = 10000.0

D_HEADS = 8
D_HEAD_DIM = BRANCH_WIDTH // D_HEADS

A_IN = 2 * BRANCH_WIDTH
B_IN = BRANCH_WIDTH + B_CONV_DIM + B_HEADS
C_IN = 2 * C_QK + 2 * BRANCH_WIDTH
D_IN = 3 * BRANCH_WIDTH + D_HEADS
G_IN = N_BRANCH * D_MODEL
N_IN = A_IN + B_IN + C_IN + D_IN + G_IN
SPLIT_POINTS = (A_IN, A_IN + B_IN, A_IN + B_IN + C_IN, A_IN + B_IN + C_IN + D_IN)

kernel_name = 'hybrid_gated_sgu_ssd_retention_fox_macaron'


def rms_norm(x, g):
    xf = x.astype(jnp.float32)
    y = xf * lax.rsqrt(jnp.mean(xf * xf, axis=-1, keepdims=True) + EPS)
    return (y * g.astype(jnp.float32)).astype(x.dtype)


def swiglu(h, w_in, w_out):
    gate, up = jnp.split(h @ w_in, 2, axis=-1)
    return (jax.nn.silu(gate) * up) @ w_out


def causal_mask():
    return jnp.tril(jnp.ones((CHUNK, CHUNK), dtype=bool))


def chunked_sgu(p, ln_g, w_s, b_s):
    bsz, seq, _ = p.shape
    nc = seq // CHUNK
    u, v = jnp.split(jax.nn.gelu(p), 2, axis=-1)
    vf = v.astype(jnp.float32)
    mu = jnp.mean(vf, axis=-1, keepdims=True)
    var = jnp.mean(jnp.square(vf - mu), axis=-1, keepdims=True)
    vn = ((vf - mu) * lax.rsqrt(var + EPS) * ln_g.astype(jnp.float32)).astype(p.dtype)
    vc = vn.reshape(bsz, nc, CHUNK, A_GROUPS, A_GROUP_DIM)
    w = jnp.where(causal_mask()[None], w_s, jnp.zeros_like(w_s))
    mixed = jnp.einsum('gts,bnsgc->bntgc', w, vc) + b_s.T[None, None, :, :, None]
    return u * mixed.reshape(bsz, seq, BRANCH_WIDTH)


def causal_depthwise_conv(x, w, b):
    k = w.shape[0]
    y = lax.conv_general_dilated(
        x, w[:, None, :].astype(x.dtype), window_strides=(1,), padding=((k - 1, 0),),
        dimension_numbers=('NWC', 'WIO', 'NWC'), feature_group_count=x.shape[-1])
    return y + b.astype(x.dtype)


def ssd_mixer(p, conv_w, conv_b, dt_bias, a_log, d_skip, norm_g):
    f32 = jnp.float32
    bsz, seq, _ = p.shape
    nc = seq // CHUNK
    hpg = B_HEADS // B_GROUPS
    z, xbc, dt = jnp.split(p, [BRANCH_WIDTH, BRANCH_WIDTH + B_CONV_DIM], axis=-1)
    xbc = jax.nn.silu(causal_depthwise_conv(xbc, conv_w, conv_b))
    xs, bm, cm = jnp.split(xbc, [BRANCH_WIDTH, BRANCH_WIDTH + B_GROUPS * B_STATE], axis=-1)
    dt = jax.nn.softplus(dt.astype(f32) + dt_bias.astype(f32))
    a = -jnp.exp(a_log.astype(f32)).reshape(B_GROUPS, hpg)
    xh = xs.astype(f32).reshape(bsz, nc, CHUNK, B_GROUPS, hpg, B_HEAD_DIM)
    dtc = dt.reshape(bsz, nc, CHUNK, B_GROUPS, hpg)
    bmc = bm.astype(f32).reshape(bsz, nc, CHUNK, B_GROUPS, B_STATE)
    cmc = cm.astype(f32).reshape(bsz, nc, CHUNK, B_GROUPS, B_STATE)
    acum = jnp.cumsum(dtc * a, axis=2)
    xdt = xh * dtc[..., None]
    seg = acum[:, :, :, None] - acum[:, :, None, :]
    mask = causal_mask()[:, :, None, None]
    decay = jnp.exp(jnp.where(mask, seg, -jnp.inf))
    cb = jnp.einsum('bctgn,bcsgn->bctsg', cmc, bmc)
    y_diag = jnp.einsum('bctsgh,bcsghp->bctghp', cb[..., None] * decay, xdt)
    decay_end = jnp.exp(acum[:, :, -1:] - acum)
    states = jnp.einsum('bcsgn,bcsghp->bcghpn', bmc, xdt * decay_end[..., None])
    chunk_decay = jnp.exp(acum[:, :, -1])

    def step(s_prev, inp):
        st, cd = inp
        return s_prev * cd[..., None, None] + st, s_prev

    init = jnp.zeros((bsz, B_GROUPS, hpg, B_HEAD_DIM, B_STATE), f32)
    _, prev = lax.scan(step, init, (jnp.moveaxis(states, 1, 0), jnp.moveaxis(chunk_decay, 1, 0)))
    prev = jnp.moveaxis(prev, 0, 1)
    y_off = jnp.einsum('bctgn,bcghpn->bctghp', cmc, prev) * jnp.exp(acum)[..., None]
    y = y_diag + y_off + xh * d_skip.astype(f32).reshape(B_GROUPS, hpg)[..., None]
    y = y.reshape(bsz, seq, BRANCH_WIDTH) * jax.nn.silu(z.astype(f32))
    yg = y.reshape(bsz, seq, B_GROUPS, BRANCH_WIDTH // B_GROUPS)
    yg = yg * lax.rsqrt(jnp.mean(yg * yg, axis=-1, keepdims=True) + EPS)
    y = yg.reshape(bsz, seq, BRANCH_WIDTH) * norm_g.astype(f32)
    return y.astype(p.dtype)


def rotary(x, pos):
    half = x.shape[-1] // 2
    inv_freq = 1.0 / (ROPE_BASE ** (jnp.arange(half, dtype=jnp.float32) / half))
    ang = pos[:, None] * inv_freq[None, :]
    cos = jnp.cos(ang)[None, :, None, :]
    sin = jnp.sin(ang)[None, :, None, :]
    x1, x2 = x[..., :half], x[..., half:]
    return jnp.concatenate([x1 * cos - x2 * sin, x1 * sin + x2 * cos], axis=-1)


def retention_mixer(p):
    f32 = jnp.float32
    bsz, seq, _ = p.shape
    nc = seq // CHUNK
    q, k, v, g = jnp.split(p, [C_QK, 2 * C_QK, 2 * C_QK + BRANCH_WIDTH], axis=-1)
    pos = jnp.arange(seq, dtype=f32)
    q = rotary(q.astype(f32).reshape(bsz, seq, C_HEADS, C_HEAD_QK), pos)
    k = rotary(k.astype(f32).reshape(bsz, seq, C_HEADS, C_HEAD_QK), pos) * (C_HEAD_QK ** -0.5)
    qc = q.reshape(bsz, nc, CHUNK, C_HEADS, C_HEAD_QK)
    kc = k.reshape(bsz, nc, CHUNK, C_HEADS, C_HEAD_QK)
    vc = v.astype(f32).reshape(bsz, nc, CHUNK, C_HEADS, C_HEAD_V)
    log_gamma = jnp.log(1.0 - 2.0 ** (-5.0 - jnp.arange(C_HEADS, dtype=f32)))
    idx = jnp.arange(CHUNK, dtype=f32)
    mask = causal_mask()
    rel = jnp.where(mask, idx[:, None] - idx[None, :], 0.0)
    intra_decay = jnp.exp(rel[..., None] * log_gamma) * mask[..., None]
    scores = jnp.einsum('bcthd,bcshd->bchts', qc, kc) * intra_decay.transpose(2, 0, 1)
    y_intra = jnp.einsum('bchts,bcshv->bcthv', scores, vc)
    k_decay = jnp.exp((CHUNK - 1.0 - idx)[:, None] * log_gamma)
    states = jnp.einsum('bcshd,bcshv->bchdv', kc * k_decay[..., None], vc)
    chunk_decay = jnp.exp(CHUNK * log_gamma)

    def step(s_prev, st):
        return s_prev * chunk_decay[:, None, None] + st, s_prev

    init = jnp.zeros((bsz, C_HEADS, C_HEAD_QK, C_HEAD_V), f32)
    _, prev = lax.scan(step, init, jnp.moveaxis(states, 1, 0))
    prev = jnp.moveaxis(prev, 0, 1)
    q_decay = jnp.exp((idx + 1.0)[:, None] * log_gamma)
    y_cross = jnp.einsum('bcthd,bchdv->bcthv', qc * q_decay[..., None], prev)
    y = y_intra + y_cross
    mu = jnp.mean(y, axis=-1, keepdims=True)
    var = jnp.mean(jnp.square(y - mu), axis=-1, keepdims=True)
    y = ((y - mu) * lax.rsqrt(var + EPS)).reshape(bsz, seq, BRANCH_WIDTH)
    return (jax.nn.silu(g.astype(f32)) * y).astype(p.dtype)


def forgetting_attention(p, f_bias):
    f32 = jnp.float32
    bsz, seq, _ = p.shape
    nb = seq // CHUNK
    q, k, v, fl = jnp.split(p, [BRANCH_WIDTH, 2 * BRANCH_WIDTH, 3 * BRANCH_WIDTH], axis=-1)
    q = q.reshape(bsz, seq, D_HEADS, D_HEAD_DIM)
    k = k.reshape(bsz, seq, D_HEADS, D_HEAD_DIM)
    v = v.reshape(bsz, seq, D_HEADS, D_HEAD_DIM)
    log_f = jax.nn.log_sigmoid(fl.astype(f32) + f_bias.astype(f32))
    cum = jnp.cumsum(log_f, axis=1).transpose(0, 2, 1)
    scale = D_HEAD_DIM ** -0.5
    kpos = jnp.arange(seq)

    def block(i):
        start = i * CHUNK
        qb = lax.dynamic_slice_in_dim(q, start, CHUNK, axis=1)
        cq = lax.dynamic_slice_in_dim(cum, start, CHUNK, axis=2)
        s = jnp.einsum('bqhd,bkhd->bhqk', qb, k).astype(f32) * scale
        s = s + cq[..., None] - cum[:, :, None, :]
        qpos = start + jnp.arange(CHUNK)
        s = jnp.where((kpos[None, :] <= qpos[:, None])[None, None], s, -jnp.inf)
        w = jax.nn.softmax(s, axis=-1)
        return jnp.einsum('bhqk,bkhd->bqhd', w.astype(v.dtype), v)

    out = lax.map(block, jnp.arange(nb))
    return out.transpose(1, 0, 2, 3, 4).reshape(bsz, seq, BRANCH_WIDTH)


def setup_inputs(seed: int = 0) -> dict:
    key = jax.random.key(seed)
    ks = jax.random.split(key, 24)
    f32 = jnp.float32

    def nrm(k, shape, scale):
        return jax.random.normal(k, shape, f32) * scale

    def gain(k, shape):
        return 1.0 + 0.05 * jax.random.normal(k, shape, f32)

    dt0 = jnp.exp(jax.random.uniform(ks[11], (DEPTH, B_HEADS), f32, math.log(1e-3), math.log(1e-1)))
    return {
        'x': jax.random.normal(ks[0], (BATCH, SEQ, D_MODEL), f32),
        'ffn1_norm': gain(ks[1], (DEPTH, D_MODEL)),
        'ffn1_w_in': nrm(ks[2], (DEPTH, D_MODEL, 2 * D_FF), D_MODEL ** -0.5),
        'ffn1_w_out': nrm(ks[3], (DEPTH, D_FF, D_MODEL), D_FF ** -0.5),
        'mix_norm': gain(ks[4], (DEPTH, D_MODEL)),
        'w_mix_in': nrm(ks[5], (DEPTH, D_MODEL, N_IN), D_MODEL ** -0.5),
        'sgu_norm': gain(ks[6], (DEPTH, BRANCH_WIDTH)),
        'sgu_w': nrm(ks[7], (DEPTH, A_GROUPS, CHUNK, CHUNK), CHUNK ** -0.5),
        'sgu_b': gain(ks[8], (DEPTH, A_GROUPS, CHUNK)),
        'conv_w': nrm(ks[9], (DEPTH, B_CONV, B_CONV_DIM), B_CONV ** -0.5),
        'conv_b': nrm(ks[10], (DEPTH, B_CONV_DIM), 0.02),
        'dt_bias': dt0 + jnp.log(-jnp.expm1(-dt0)),
        'a_log': jnp.log(jax.random.uniform(ks[12], (DEPTH, B_HEADS), f32, 1.0, 16.0)),
        'd_skip': gain(ks[13], (DEPTH, B_HEADS)),
        'ssm_norm': gain(ks[14], (DEPTH, BRANCH_WIDTH)),
        'forget_bias': jax.random.uniform(ks[15], (DEPTH, D_HEADS), f32, 2.0, 5.0),
        'w_branch': nrm(ks[16], (DEPTH, N_BRANCH, BRANCH_WIDTH, D_MODEL), BRANCH_WIDTH ** -0.5),
        'w_mix_out': nrm(ks[17], (DEPTH, D_MODEL, D_MODEL), D_MODEL ** -0.5),
        'ffn2_norm': gain(ks[18], (DEPTH, D_MODEL)),
        'ffn2_w_in': nrm(ks[19], (DEPTH, D_MODEL, 2 * D_FF), D_MODEL ** -0.5),
        'ffn2_w_out': nrm(ks[20], (DEPTH, D_FF, D_MODEL), D_FF ** -0.5),
        'final_norm': gain(ks[21], (D_MODEL,)),
    }


def reference(x, ffn1_norm, ffn1_w_in, ffn1_w_out, mix_norm, w_mix_in, sgu_norm, sgu_w, sgu_b,
              conv_w, conv_b, dt_bias, a_log, d_skip, ssm_norm, forget_bias, w_branch, w_mix_out,
              ffn2_norm, ffn2_w_in, ffn2_w_out, final_norm):
    for l in range(DEPTH):
        x = x + 0.5 * swiglu(rms_norm(x, ffn1_norm[l]), ffn1_w_in[l], ffn1_w_out[l])
        h = rms_norm(x, mix_norm[l])
        p_a, p_b, p_c, p_d, p_g = jnp.split(h @ w_mix_in[l], SPLIT_POINTS, axis=-1)
        y_a = chunked_sgu(p_a, sgu_norm[l], sgu_w[l], sgu_b[l])
        y_b = ssd_mixer(p_b, conv_w[l], conv_b[l], dt_bias[l], a_log[l], d_skip[l], ssm_norm[l])
        y_c = retention_mixer(p_c)
        y_d = forgetting_attention(p_d, forget_bias[l])
        gates = jax.nn.sigmoid(p_g.astype(jnp.float32)).astype(x.dtype)
        merged = gates[..., :D_MODEL] * (y_a @ w_branch[l, 0])
        merged = merged + gates[..., D_MODEL:2 * D_MODEL] * (y_b @ w_branch[l, 1])
        merged = merged + gates[..., 2 * D_MODEL:3 * D_MODEL] * (y_c @ w_branch[l, 2])
        merged = merged + gates[..., 3 * D_MODEL:] * (y_d @ w_branch[l, 3])
        x = x + merged @ w_mix_out[l]
        x = x + 0.5 * swiglu(rms_norm(x, ffn2_norm[l]), ffn2_w_in[l], ffn2_w_out[l])
    return rms_norm(x, final_norm)
```

```python
import math
import numpy as np
import concourse.bass as bass
import concourse.mybir as mybir
from concourse.bass_utils import run_bass_kernel_spmd

F32, BF16 = mybir.dt.float32, mybir.dt.bfloat16
AF = mybir.ActivationFunctionType
ALU = mybir.AluOpType
AX = mybir.AxisListType
P = 128
SB_BASE = 16512
SB_END = 229376
EPS = 1e-6
NCORES = 8


class Cfg:
    def __init__(self, L=2048, D=2048, DFF=5632, depth=2):
        self.L, self.D, self.DFF, self.depth = L, D, DFF, depth
        self.LT = L // 2
        self.KC = D // P
        self.NB = L // P
        self.NBL = self.LT // P
        self.TP = min(1024, self.LT)
        o = [0]

        def take(n):
            r = o[0]
            o[0] += n
            return r
        self.o_au, self.o_av = take(1024), take(1024)
        self.o_bz, self.o_bx, self.o_bdt = take(512), take(768), take(8)
        self.o_cq, self.o_ck, self.o_cv, self.o_cg = take(256), take(256), take(512), take(512)
        self.o_dq, self.o_dk, self.o_dv, self.o_df = take(512), take(512), take(512), take(4)
        self.o_g = take(4 * D)
        self.NSEL = o[0]
        self.NV = 4 * self.KC + 24 + 6 + 4 + 4 + 3 + 64 + 2


class Builder:
    def __init__(self, cfg, debug=False, ncores=8):
        self.c = cfg
        self.debug = debug
        self.ncores = ncores
        nc = bass.Bass("TRN2", target_bir_lowering=False)
        self.nc = nc
        self.eng = {"pe": nc.tensor, "act": nc.scalar, "dve": nc.vector, "pool": nc.gpsimd, "sp": nc.sync}
        self.esem = {k: nc.alloc_semaphore("es_" + k) for k in ["pe", "act", "dve", "pool"]}
        self.ecnt = {k: 0 for k in self.esem}
        self.seen = {k: {} for k in self.eng}
        self.lastw = {}
        self.readers = {}
        self.dsl = {q: [nc.alloc_semaphore("ds_%s%d" % (q, i)) for i in range(8)] for q in ["sp", "pool"]}
        self.dcnt = {q: [0] * 8 for q in self.dsl}
        self.dnext = {q: 0 for q in self.dsl}
        self.sb_off = SB_BASE
        self.uid = 0
        self.psb = [nc.alloc_psum_tensor("psb%d" % i, [P, 512], F32) for i in range(8)]

    def _wait(self, ek, ev):
        sem, val, _ = ev
        if self.seen[ek].get(sem.name, 0) < val:
            self.eng[ek].wait_ge(sem, val)
            self.seen[ek][sem.name] = val

    def _gather(self, ek, reads, writes, is_dma):
        deps = []
        for r in reads:
            w = self.lastw.get(r)
            if w is not None:
                deps.append(w)
        for w_ in writes:
            w = self.lastw.get(w_)
            if w is not None:
                deps.append(w)
            for rd in self.readers.get(w_, ()):
                if is_dma or rd[2] != ek:
                    deps.append(rd)
        for ev in deps:
            if (not is_dma) and ek == "pe" and ev[2] == "pe":
                continue
            self._wait(ek, ev)

    def _record(self, ev, reads, writes):
        for w_ in writes:
            self.lastw[w_] = ev
            self.readers[w_] = []
        for r in reads:
            lst = self.readers.setdefault(r, [])
            lst[:] = [x for x in lst if x[0].name != ev[0].name]
            lst.append(ev)

    def op(self, ek, reads, writes, fn):
        self._gather(ek, reads, writes, False)
        ins = fn()
        self.ecnt[ek] += 1
        ins.then_inc(self.esem[ek], 1)
        self._record((self.esem[ek], self.ecnt[ek], ek), reads, writes)

    def dma(self, q, reads, writes, out, in_):
        self._gather(q, reads, writes, True)
        i = self.dnext[q]
        self.dnext[q] = (i + 1) % 8
        sem = self.dsl[q][i]
        if self.dcnt[q][i] > 0:
            self._wait(q, (sem, 16 * self.dcnt[q][i], "dma"))
        if self.dcnt[q][i] >= 48:
            self.nsem_retired = getattr(self, "nsem_retired", 0) + 1
            sem = self.nc.alloc_semaphore("ds_%s%d_r%d" % (q, i, self.nsem_retired))
            self.dsl[q][i] = sem
            self.dcnt[q][i] = 0
        ins = self.eng[q].dma_start(out=out, in_=in_)
        self.dcnt[q][i] += 1
        ins.then_inc(sem, 16)
        self._record((sem, 16 * self.dcnt[q][i], "dma_" + q), reads, writes)

    def barrier(self):
        evs = [(self.esem[k], self.ecnt[k], k) for k in self.esem if self.ecnt[k] > 0]
        for q in self.dsl:
            for i in range(8):
                if self.dcnt[q][i] > 0:
                    evs.append((self.dsl[q][i], 16 * self.dcnt[q][i], "dma"))
        for ek in self.eng:
            for ev in evs:
                self._wait(ek, ev)
        self.lastw.clear()
        self.readers.clear()

    def sb(self, shape, dt, name):
        nbytes = int(np.prod(shape[1:])) * (4 if dt == F32 else 2)
        nbytes = (nbytes + 63) // 64 * 64
        assert self.sb_off + nbytes <= SB_END, ("SBUF overflow", name, self.sb_off, nbytes)
        self.uid += 1
        t = self.nc.alloc_sbuf_tensor_at("%s_%d" % (name, self.uid), list(shape), dt, offset=self.sb_off)
        self.sb_off += nbytes
        return t

    def dram(self, name, shape, dt):
        kind = "ExternalOutput" if self.debug else "Internal"
        return self.nc.dram_tensor(name, list(shape), dt, kind=kind).ap()

    def mm(self, out, lhsT, rhs, start, stop, reads, writes):
        self.op("pe", reads, writes, lambda: self.nc.tensor.matmul(out, lhsT, rhs, start=start, stop=stop))

    def act(self, out, in_, func, reads, writes, bias=None, scale=None):
        kw = {}
        if bias is not None:
            kw["bias"] = bias
        if scale is not None:
            kw["scale"] = scale
        self.op("act", reads, writes, lambda: self.nc.scalar.activation(out, in_, func, **kw))

    def rsqrt(self, out, in_, mult, add, reads, writes):
        self.act(out, in_, AF.Ln, reads, writes, bias=float(add), scale=float(mult))
        self.act(out, out, AF.Exp, writes, writes, scale=-0.5)

    def ts(self, out, in0, s1, s2, op0, op1, reads, writes, eng="dve"):
        e = self.nc.vector if eng == "dve" else self.nc.gpsimd
        if s2 is None:
            self.op(eng, reads, writes, lambda: e.tensor_scalar(out, in0, s1, None, op0))
        else:
            self.op(eng, reads, writes, lambda: e.tensor_scalar(out, in0, s1, s2, op0, op1))

    def stt(self, out, in0, s, in1, op0, op1, reads, writes, eng="dve"):
        e = self.nc.vector if eng == "dve" else self.nc.gpsimd
        self.op(eng, reads, writes, lambda: e.scalar_tensor_tensor(out, in0, s, in1, op0, op1))

    def tt(self, out, in0, in1, op, reads, writes, eng="dve"):
        e = self.nc.vector if eng == "dve" else self.nc.gpsimd
        self.op(eng, reads, writes, lambda: e.tensor_tensor(out, in0, in1, op))

    def declare_io(self):
        c, nc = self.c, self.nc
        dep = c.depth

        def inp(name, shape, dt=F32):
            return nc.dram_tensor(name, list(shape), dt, kind="ExternalInput").ap()

        def internal(name, shape, dt):
            return nc.dram_tensor(name, list(shape), dt, kind="Internal").ap()

        self.i_xT = inp("xT", [c.D, c.LT])
        self.i_w1a = inp("w1a", [dep, c.D, 2 * c.DFF])
        self.i_w1b = inp("w1b", [dep, c.DFF, c.D])
        self.i_w2a = inp("w2a", [dep, c.D, 2 * c.DFF])
        self.i_w2b = inp("w2b", [dep, c.DFF, c.D])
        self.i_wsel = [inp("wsel%d" % l, [c.D, c.NSEL]) for l in range(dep)]
        self.i_wrot = inp("wrot", [dep, c.D, 512])
        self.i_wbr = inp("wbr", [dep, 4, 1024, c.D])
        self.i_wout = inp("wout", [dep, c.D, c.D])
        self.i_vecs = inp("vecs", [dep, P, c.NV])
        self.i_lng = inp("lng", [dep, 1, 1024])
        self.i_sgub = inp("sgub", [dep, 1, 1024])
        self.i_sguwT = inp("sguwT", [dep, P, 8, P])
        self.i_consts = inp("consts", [P, 2 * P + 16 * P])
        self.i_rope = inp("rope", [P, 2 * c.L])
        self.i_dec = inp("dec", [P, 4 * 640])
        self.o_outT = nc.dram_tensor("outT", [c.D, c.LT], F32, kind="ExternalOutput").ap()
        self.x_d = self.dram("x_d", [c.D, c.LT], F32)
        self.hm_d = internal("hm_d", [c.D, c.LT], BF16)
        self.hm_g = internal("hm_g", [2 * c.D, c.LT], BF16)
        self.ya_d = self.dram("ya_d", [1024, c.LT], BF16)
        self.ysend = internal("ysend", [2 * 1536, c.LT], BF16)
        self.yrecv = internal("yrecv", [4 * 1536, c.LT], BF16)
        self.pa_u = self.dram("pa_u", [1024, c.LT], BF16)
        self.pa_v = self.dram("pa_v", [c.LT, 1024], BF16)
        self.pb_z = self.dram("pb_z", [512, c.L], BF16)
        self.pb_x = self.dram("pb_x", [768, c.L], BF16)
        self.pc_q = self.dram("pc_q", [256, c.L], BF16)
        self.pc_k = self.dram("pc_k", [256, c.L], BF16)
        self.pc_v = self.dram("pc_v", [c.L, 512], BF16)
        self.pc_g = self.dram("pc_g", [512, c.L], BF16)
        self.pd_q = self.dram("pd_q", [512, c.L], BF16)
        self.pd_k = self.dram("pd_k", [512, c.L], BF16)
        self.pd_v = self.dram("pd_v", [c.L, 512], BF16)

    def allgather(self, src, dst, rpp):
        self.barrier()
        self.ncc = getattr(self, "ncc", 0) + 1
        sem = self.nc.alloc_semaphore("cc%d" % self.ncc)
        groups = [[2 * i, 2 * i + 1] for i in range(self.ncores // 2)]
        npc = src.shape[0] // rpp
        for pc in range(npc):
            ins = self.nc.gpsimd.collective_compute("AllGather", ALU.bypass, replica_groups=groups,
                                                    ins=[src[pc * rpp:(pc + 1) * rpp, :].opt()],
                                                    outs=[dst[pc * 2 * rpp:(pc + 1) * 2 * rpp, :].opt()])
            ins.then_inc(sem)
            self.nc.gpsimd.wait_ge(sem, pc + 1)
        self.seen["pool"][sem.name] = npc
        for ek in self.eng:
            self._wait(ek, (sem, npc, "cc"))

    def alloc_global(self):
        c = self.c
        self.g_consts = self.sb([P, 2 * P + 16 * P], F32, "consts")
        self.g_vecs = [self.sb([P, c.NV], F32, "vecs%d" % l) for l in range(c.depth)]
        self.g_onesf = self.sb([P, P], F32, "onesf")
        self.g_onesb = self.sb([P, P], BF16, "onesb")
        self.g_identb = self.sb([P, P], BF16, "identb")
        self.g_maskb = self.sb([P, P], BF16, "maskb")
        self.g_dt = self.sb([P, c.L], F32, "dtraw")
        self.g_fl = self.sb([P, c.L], F32, "flraw")
        self.sb_phase = self.sb_off
        self.op("dve", [], ["flraw"], lambda: self.nc.vector.memset(self.g_fl[0:8, :], 0.0))
        self.dma("sp", [], ["consts"], self.g_consts[:, :], self.i_consts[:, :])
        for l in range(c.depth):
            self.dma("sp", [], ["vecs%d" % l], self.g_vecs[l][:, :], self.i_vecs[l, :, :])
        self.op("dve", [], ["onesf"], lambda: self.nc.vector.memset(self.g_onesf[:, :], 1.0))
        self.op("dve", [], ["onesb"], lambda: self.nc.vector.memset(self.g_onesb[:, :], 1.0))
        self.op("dve", ["consts"], ["identb"],
                lambda: self.nc.vector.tensor_copy(self.g_identb[:, :], self.g_consts[:, P:2 * P]))
        self.op("dve", ["consts"], ["maskb"],
                lambda: self.nc.vector.tensor_copy(self.g_maskb[:, :], self.g_consts[:, 0:P]))
        self.cmask = self.g_consts[:, 0:P]
        self.identf = self.g_consts[:, P:2 * P]

    def sel(self, h):
        return self.g_consts[0:16, 2 * P + h * P: 2 * P + (h + 1) * P]

    def vec(self, l, name, j=0):
        c = self.c
        k3 = 3 * c.KC
        offs = {"n1": 0, "nm": c.KC, "n2": 2 * c.KC, "cw": k3, "cb": k3 + 24, "dsk": k3 + 30, "ssm": k3 + 34,
                "dtb": k3 + 38, "alog": k3 + 39, "fb": k3 + 40, "nf": k3 + 41, "gpw": k3 + 41 + c.KC,
                "f0": k3 + 41 + c.KC + 64, "f1": k3 + 41 + c.KC + 65}
        o = offs[name] + j
        return self.g_vecs[l][:, o:o + 1]

    def phase_begin(self):
        self.barrier()
        self.epoch = getattr(self, "epoch", 0) + 1
        self.esem = {k: self.nc.alloc_semaphore("es%d_%s" % (self.epoch, k)) for k in ["pe", "act", "dve", "pool"]}
        self.ecnt = {k: 0 for k in self.esem}
        self.sb_off = self.sb_phase
        self.ring = None

    def make_ring(self, n):
        self.ring = [self.sb([P, 8192], BF16, "ring%d" % i) for i in range(n)]
        self.ring_i = 0

    def load_w(self, src, kc, ncol):
        i = self.ring_i
        self.ring_i = (i + 1) % len(self.ring)
        assert kc * ncol <= 8192
        view = self.ring[i][:, 0:kc * ncol].rearrange("p (k n) -> p k n", k=kc)
        self.dma("pool", [], [("ring", i)], view, src.rearrange("(k p) n -> p k n", p=P))
        return ("ring", i), view

    def rmsnorm(self, xT, xres, gname, l, hT, hres, T, tmp, out_f32=False):
        c = self.c
        sq, rstd = tmp
        for tg in range(T // 512):
            tsl = slice(tg * 512, (tg + 1) * 512)
            ps = self.psb[7]
            for k in range(c.KC):
                j = k % 2
                self.act(sq[j][:, :], xT[:, k, tsl], AF.Square, [xres], [("sq", j)])
                self.mm(ps[:, :], self.g_onesf[:, :], sq[j][:, :], k == 0, k == c.KC - 1,
                        [("sq", j), "onesf"], [("ps", 7)])
            self.rsqrt(rstd[:, :], ps[:, :], 1.0 / c.D, EPS, [("ps", 7)], ["rstd"])
            for k in range(c.KC):
                self.stt(hT[:, k, tsl], xT[:, k, tsl], self.vec(l, gname, k), rstd[:, :], ALU.mult, ALU.mult,
                         [xres, "rstd", "vecs%d" % l], [hres])

    def ffn(self, l, w_in, w_out, xT, hT, hid, sil, T):
        c = self.c
        G = 4
        ngr = c.DFF // (G * P)
        pi = 0
        for gi in range(ngr):
            rg, wg = self.load_w(w_in[:, gi * 512:(gi + 1) * 512], c.KC, 512)
            ru, wu = self.load_w(w_in[:, c.DFF + gi * 512: c.DFF + (gi + 1) * 512], c.KC, 512)
            hb = gi % 2
            for fc in range(G):
                for tg in range(T // 512):
                    tsl = slice(tg * 512, (tg + 1) * 512)
                    bg, bu = pi % 6, (pi + 1) % 6
                    pi += 2
                    for k in range(c.KC):
                        self.mm(self.psb[bg][:, :], wg[:, k, fc * P:(fc + 1) * P], hT[:, k, tsl], k == 0, k == c.KC - 1,
                                [rg, "hT"], [("ps", bg)])
                    for k in range(c.KC):
                        self.mm(self.psb[bu][:, :], wu[:, k, fc * P:(fc + 1) * P], hT[:, k, tsl], k == 0, k == c.KC - 1,
                                [ru, "hT"], [("ps", bu)])
                    sj = pi % 4 // 2
                    self.act(sil[sj][:, :], self.psb[bg][:, :], AF.Silu, [("ps", bg)], [("sil", sj)])
                    self.tt(hid[hb][:, fc, tsl], sil[sj][:, :], self.psb[bu][:, :], ALU.mult,
                            [("sil", sj), ("ps", bu)], [("hid", hb)])
            r2, w2 = self.load_w(w_out[gi * 512:(gi + 1) * 512, :], G, c.D)
            for dc in range(c.KC):
                for tg in range(T // 512):
                    tsl = slice(tg * 512, (tg + 1) * 512)
                    b = 6 + (pi % 2)
                    if b == 7:
                        b = 6
                    b = pi % 6
                    pi += 1
                    for g in range(G):
                        self.mm(self.psb[b][:, :], w2[:, g, dc * P:(dc + 1) * P], hid[hb][:, g, tsl], g == 0, g == G - 1,
                                [r2, ("hid", hb)], [("ps", b)])
                    self.stt(xT[:, dc, tsl], self.psb[b][:, :], 0.5, xT[:, dc, tsl], ALU.mult, ALU.add,
                             [("ps", b), "xT"], ["xT"])

    def ffn_phase(self, l, which, src_x, first_norm, post):
        c = self.c
        T = c.TP
        self.phase_begin()
        xT = self.sb([P, c.KC, T], F32, "xT")
        ovl0 = self.sb_off
        hT = self.sb([P, c.KC, T], BF16, "hT")
        hid = [self.sb([P, 4, T], BF16, "hid%d" % i) for i in range(2)]
        self.make_ring(3)
        ovl1 = self.sb_off
        sq = [self.sb([P, 512], F32, "sq%d" % i) for i in range(2)]
        rstd = self.sb([P, 512], F32, "rstd")
        sil = [self.sb([P, 512], F32, "sil%d" % i) for i in range(2)]
        save = self.sb_off
        self.sb_off = ovl0
        outF = self.sb([P, c.KC, T], F32, "outF") if post == "final" else None
        assert self.sb_off <= ovl1 or post != "final"
        self.sb_off = save
        w_in = (self.i_w1a if which == 1 else self.i_w2a)[l]
        w_out = (self.i_w1b if which == 1 else self.i_w2b)[l]
        xd = src_x.rearrange("(k p) t -> p k t", p=P)
        for ps_ in range(c.LT // T):
            tok = slice(ps_ * T, (ps_ + 1) * T)
            self.dma("sp", [], ["xT"], xT[:, :, :], xd[:, :, tok])
            self.rmsnorm(xT, "xT", "n1" if which == 1 else "n2", l, hT, "hT", T, (sq, rstd))
            self.ffn(l, w_in, w_out, xT, hT, hid, sil, T)
            if post == "ffn1next":
                self.rmsnorm(xT, "xT", "n1", l + 1, hT, "hT", T, (sq, rstd))
                self.ffn(l + 1, self.i_w1a[l + 1], self.i_w1b[l + 1], xT, hT, hid, sil, T)
                post_l = l + 1
            else:
                post_l = l
            if post in ("mix", "ffn1next"):
                self.rmsnorm(xT, "xT", "nm", post_l, hT, "hT", T, (sq, rstd))
                self.dma("sp", ["hT"], [], self.hm_d.rearrange("(k p) t -> p k t", p=P)[:, :, tok], hT[:, :, :])
                self.dma("sp", ["xT"], [], self.x_d.rearrange("(k p) t -> p k t", p=P)[:, :, tok], xT[:, :, :])
            else:
                self.barrier()
                self.rmsnorm(xT, "xT", "nf", l, outF, "outF", T, (sq, rstd))
                self.dma("sp", ["outF"], [], self.o_outT.rearrange("(k p) t -> p k t", p=P)[:, :, tok], outF[:, :, :])
                self.barrier()
        if post in ("mix", "ffn1next"):
            self.allgather(self.hm_d, self.hm_g, min(512, self.c.D))

    def proj_phase(self, l):
        c = self.c
        L, LT, KC = c.L, c.LT, c.KC
        self.phase_begin()
        hm = self.sb([P, KC, L], BF16, "hm")
        self.make_ring(3)
        lng = self.sb([P, 1024], F32, "lng")
        stg = [self.sb([P, L], BF16, "stg%d" % i) for i in range(2)]
        vrow = self.sb([P, 1024], F32, "vrow")
        vcen = self.sb([P, 1024], F32, "vcen")
        vsq = self.sb([P, 1024], F32, "vsq")
        vst = [self.sb([P, 1024], BF16, "vst%d" % i) for i in range(2)]
        sm = self.sb([P, 8], F32, "sm")
        t1 = [self.sb([P, 512], F32, "t1_%d" % i) for i in range(2)]
        t2 = [self.sb([P, 512], F32, "t2_%d" % i) for i in range(2)]
        ovl = self.sb_off
        xraw = self.sb([P, L + 4], F32, "xraw")
        cacc = self.sb([P, L], F32, "cacc")
        self.sb_off = ovl
        rope = self.sb([P, 2 * L], F32, "rope")
        self.dma("sp", [], ["lng"], lng[:, :], self.i_lng[l, :, :].partition_broadcast(P))
        self.op("dve", [], ["xraw"], lambda: self.nc.vector.memset(xraw[:, 0:4], 0.0))
        wm = self.i_wsel[l]
        st = {"pi": 0}

        def fm_section(col0, ncols, epi, ntok, M=P):
            nslot = (ncols + 511) // 512
            for s in range(nslot):
                n = min(512, ncols - s * 512)
                r, wv = self.load_w(wm[:, col0 + s * 512: col0 + s * 512 + n], KC, n)
                for cbi in range((n + M - 1) // M):
                    cb = s * 4 + cbi
                    for tg in range(ntok // 512):
                        b = st["pi"] % 6
                        st["pi"] += 1
                        for k in range(KC):
                            self.mm(self.psb[b][0:M, :], wv[:, k, cbi * M:(cbi + 1) * M], hm[:, k, tg * 512:(tg + 1) * 512],
                                    k == 0, k == KC - 1, [r, "hm"], [("ps", b)])
                        epi(cb, tg, b)

        def store_fm(dst, func, ntok):
            def epi(cb, tg, b):
                j = cb % 2
                self.act(stg[j][:, tg * 512:(tg + 1) * 512], self.psb[b][:, :], func, [("ps", b)], [("stg", j)])
                if tg == ntok // 512 - 1:
                    self.dma("sp", [("stg", j)], [], dst[cb * P:(cb + 1) * P, :], stg[j][:, 0:ntok])
            return epi

        def tm_section(col0, ncols, dst, ln, ntok):
            slots = [self.load_w(wm[:, col0 + h * 512: col0 + (h + 1) * 512], KC, 512) for h in range(ncols // 512)]
            for tb in range(ntok // P):
                bs = []
                for (r, wv) in slots:
                    b = st["pi"] % 6
                    st["pi"] += 1
                    bs.append(b)
                    for k in range(KC):
                        self.mm(self.psb[b][:, :], hm[:, k, tb * P:(tb + 1) * P], wv[:, k, :], k == 0, k == KC - 1,
                                [r, "hm"], [("ps", b)])
                j = tb % 2
                if not ln:
                    for half in range(len(slots)):
                        self.act(vst[j][:, half * 512:(half + 1) * 512], self.psb[bs[half]][:, :], AF.Copy,
                                 [("ps", bs[half])], [("vst", j)])
                else:
                    for half in range(2):
                        self.act(vrow[:, half * 512:(half + 1) * 512], self.psb[bs[half]][:, :], AF.Gelu_apprx_tanh,
                                 [("ps", bs[half])], ["vrow"])
                    self.op("dve", ["vrow"], ["sm0"], lambda: self.nc.vector.reduce_sum(sm[:, 0:1], vrow[:, :], axis=AX.X))
                    self.ts(sm[:, 1:2], sm[:, 0:1], -1.0 / 1024, None, ALU.mult, None, ["sm0"], ["sm1"])
                    self.ts(vcen[:, :], vrow[:, :], sm[:, 1:2], None, ALU.add, None, ["vrow", "sm1"], ["vcen"])
                    self.tt(vsq[:, :], vcen[:, :], vcen[:, :], ALU.mult, ["vcen"], ["vsq"])
                    self.op("dve", ["vsq"], ["sm2"], lambda: self.nc.vector.reduce_sum(sm[:, 2:3], vsq[:, :], axis=AX.X))
                    self.rsqrt(sm[:, 4:5], sm[:, 2:3], 1.0 / 1024, EPS, ["sm2"], ["sm4"])
                    self.stt(vst[j][:, :], vcen[:, :], sm[:, 4:5], lng[:, :], ALU.mult, ALU.mult,
                             ["vcen", "sm4", "lng"], [("vst", j)])
                self.dma("sp", [("vst", j)], [], dst[tb * P:(tb + 1) * P, :], vst[j][:, 0:ncols])

        self.dma("sp", [], ["hm"], hm[:, :, 0:LT], self.hm_d.rearrange("(k p) t -> p k t", p=P))
        fm_section(c.o_au, 1024, store_fm(self.pa_u, AF.Gelu_apprx_tanh, LT), LT)
        tm_section(c.o_av, 1024, self.pa_v, True, LT)
        self.barrier()
        rpp = min(512, c.D)
        kpp = rpp // P
        for pc in range(c.D // rpp):
            for r in range(2):
                row0 = (pc * 2 + r) * rpp
                self.dma("sp", [], ["hm"], hm[:, pc * kpp:(pc + 1) * kpp, r * LT:(r + 1) * LT],
                         self.hm_g[row0:row0 + rpp, :].rearrange("(k p) t -> p k t", p=P))
        fm_section(c.o_bz, 512, store_fm(self.pb_z, AF.Silu, L), L)

        def epi_xbc(cb, tg, b):
            j = cb % 2
            self.act(xraw[:, 4 + tg * 512: 4 + (tg + 1) * 512], self.psb[b][:, :], AF.Copy, [("ps", b)], ["xraw"])
            if tg == L // 512 - 1:
                vr = ["vecs%d" % l]
                self.ts(cacc[:, :], xraw[:, 1:1 + L], self.vec(l, "cw", cb), self.vec(l, "cb", cb),
                        ALU.mult, ALU.add, ["xraw"] + vr, ["cacc"])
                for kk in range(1, 4):
                    self.stt(cacc[:, :], xraw[:, 1 + kk:1 + kk + L], self.vec(l, "cw", kk * 6 + cb), cacc[:, :],
                             ALU.mult, ALU.add, ["xraw", "cacc"] + vr, ["cacc"])
                self.act(stg[j][:, :], cacc[:, :], AF.Silu, ["cacc"], [("stg", j)])
                self.dma("sp", [("stg", j)], [], self.pb_x[cb * P:(cb + 1) * P, :], stg[j][:, :])
        fm_section(c.o_bx, 768, epi_xbc, L)

        def epi_small(dst, res, M):
            def epi(cb, tg, b):
                self.act(dst[0:M, tg * 512:(tg + 1) * 512], self.psb[b][0:M, :], AF.Copy, [("ps", b)], [res])
            return epi
        fm_section(c.o_bdt, 8, epi_small(self.g_dt, "dtraw", 8), L, M=8)
        fm_section(c.o_df, 4, epi_small(self.g_fl, "flraw", 4), L, M=4)
        self.barrier()
        self.dma("sp", [], ["rope"], rope[:, :], self.i_rope[:, :])

        def rot_section(col0, rcol0, dst):
            r, wv = self.load_w(wm[:, col0: col0 + 256], KC, 256)
            r2, wv2 = self.load_w(self.i_wrot[l][:, rcol0: rcol0 + 256], KC, 256)
            for cb in range(2):
                j = cb % 2
                for tg in range(L // 512):
                    tsl = slice(tg * 512, (tg + 1) * 512)
                    b, b2 = st["pi"] % 6, (st["pi"] + 1) % 6
                    st["pi"] += 2
                    for k in range(KC):
                        self.mm(self.psb[b][:, :], wv[:, k, cb * P:(cb + 1) * P], hm[:, k, tsl], k == 0, k == KC - 1,
                                [r, "hm"], [("ps", b)])
                    for k in range(KC):
                        self.mm(self.psb[b2][:, :], wv2[:, k, cb * P:(cb + 1) * P], hm[:, k, tsl], k == 0, k == KC - 1,
                                [r2, "hm"], [("ps", b2)])
                    jj = tg % 2
                    self.tt(t1[jj][:, :], self.psb[b][:, :], rope[:, tsl], ALU.mult, [("ps", b), "rope"], [("t1", jj)])
                    self.tt(t2[jj][:, :], self.psb[b2][:, :], rope[:, L + tg * 512: L + (tg + 1) * 512], ALU.mult,
                            [("ps", b2), "rope"], [("t2", jj)])
                    self.tt(stg[j][:, tsl], t1[jj][:, :], t2[jj][:, :], ALU.add, [("t1", jj), ("t2", jj)], [("stg", j)],
                            eng="pool")
                self.dma("sp", [("stg", j)], [], dst[cb * P:(cb + 1) * P, :], stg[j][:, :])
        rot_section(c.o_cq, 0, self.pc_q)
        rot_section(c.o_ck, 256, self.pc_k)
        tm_section(c.o_cv, 512, self.pc_v, False, L)
        fm_section(c.o_cg, 512, store_fm(self.pc_g, AF.Silu, L), L)
        fm_section(c.o_dq, 512, store_fm(self.pd_q, AF.Copy, L), L)
        fm_section(c.o_dk, 512, store_fm(self.pd_k, AF.Copy, L), L)
        tm_section(c.o_dv, 512, self.pd_v, False, L)

    def mixer_phase(self, l):
        c = self.c
        L, NB, LT, NBL = c.L, c.NB, c.LT, c.NBL
        NG = L // 512
        self.phase_begin()
        nc = self.nc
        vr = ["vecs%d" % l]
        dec = self.sb([P, 4 * 640], F32, "dec")
        self.dma("sp", [], ["dec"], dec[:, :], self.i_dec[:, :])
        sm16 = self.sb([P, 8], F32, "sm16")
        acf = self.sb([P, L], F32, "acf")
        cuf = self.sb([P, L], F32, "cuf")
        dt_tm = self.sb([P, NB, 8], F32, "dt_tm")
        nac_tm = self.sb([P, NB, 8], F32, "nac_tm")
        ncu_tm = self.sb([P, NB, 8], F32, "ncu_tm")
        bc = self.sb([P, L], F32, "bc")
        bc2 = self.sb([P, L], F32, "bc2")
        Kt = self.sb([P, L], BF16, "Kt")
        Qt = self.sb([P, L], BF16, "Qt")
        Vt = self.sb([P, NB, P], BF16, "Vt")
        Gt = self.sb([P, L], BF16, "Gt")
        yst = [self.sb([P, L], BF16, "yst%d" % i) for i in range(2)]
        tmpf = [self.sb([P, 512], F32, "tmpf%d" % i) for i in range(2)]
        tmpe = [self.sb([P, 512], F32, "tmpe%d" % i) for i in range(2)]
        Pt = [self.sb([P, 512], BF16, "Pt%d" % i) for i in range(3)]
        of = self.sb([P, 512], F32, "of")
        o2 = self.sb([P, 512], F32, "o2")
        mean = self.sb([P, 512], F32, "mean")
        rs = self.sb([P, 512], F32, "rs")
        save = self.sb_off
        dtf = self.sb([P, L], F32, "dtf")
        dA = self.sb([P, L], F32, "dA")
        onesL = self.sb([P, L], F32, "onesL")
        st = {"p": 0, "s": 0}

        def ysend_rows(row0, src, src_res):
            for h in range(2):
                self.dma("sp", [src_res], [], self.ysend[h * 1536 + row0: h * 1536 + row0 + P, :], src[:, h * LT:(h + 1) * LT])

        self.op("dve", [], ["onesL"], lambda: nc.vector.memset(onesL[:, :], 1.0))
        self.act(sm16[0:8, 0:1], self.vec(l, "alog")[0:8, :], AF.Exp, vr, ["sm_a"])
        self.ts(sm16[0:8, 1:2], sm16[0:8, 0:1], -1.0, None, ALU.mult, None, ["sm_a"], ["sm_na"])
        self.act(dtf[0:8, :], self.g_dt[0:8, :], AF.Softplus, ["dtraw"] + vr, ["dtf"], bias=self.vec(l, "dtb")[0:8, :])
        self.ts(dA[0:8, :], dtf[0:8, :], sm16[0:8, 1:2], None, ALU.mult, None, ["dtf", "sm_na"], ["dA"])
        self.op("dve", ["dA", "onesL"], ["acf"],
                lambda: nc.vector.tensor_tensor_scan(acf[0:8, :], onesL[0:8, :], dA[0:8, :], 0.0, ALU.mult, ALU.add))
        self.ts(sm16[0:8, 2:3], self.vec(l, "fb")[0:8, :], -1.0, None, ALU.mult, None, vr, ["sm_nfb"])
        self.act(dA[0:8, :], self.g_fl[0:8, :], AF.Softplus, ["flraw", "sm_nfb", "acf"], ["dA"], bias=sm16[0:8, 2:3], scale=-1.0)
        self.ts(dA[0:8, :], dA[0:8, :], -1.0, None, ALU.mult, None, ["dA"], ["dA"])
        self.op("dve", ["dA", "onesL"], ["cuf"],
                lambda: nc.vector.tensor_tensor_scan(cuf[0:8, :], onesL[0:8, :], dA[0:8, :], 0.0, ALU.mult, ALU.add))
        for blk in range(NB):
            bsl = slice(blk * P, (blk + 1) * P)
            b = 6
            self.op("pe", ["dtf", "consts"], [("ps", b)],
                    lambda: nc.tensor.transpose(self.psb[b][:, 0:8], dtf[0:8, bsl], self.identf[0:8, 0:8]))
            self.op("pe", ["acf", "consts"], [("ps", b)],
                    lambda: nc.tensor.transpose(self.psb[b][:, 16:24], acf[0:8, bsl], self.identf[0:8, 0:8]))
            self.op("pe", ["cuf", "consts"], [("ps", b)],
                    lambda: nc.tensor.transpose(self.psb[b][:, 32:40], cuf[0:8, bsl], self.identf[0:8, 0:8]))
            self.ts(dt_tm[:, blk, :], self.psb[b][:, 0:8], 1.0, None, ALU.mult, None, [("ps", b)], ["dt_tm"])
            self.ts(nac_tm[:, blk, :], self.psb[b][:, 16:24], -1.0, None, ALU.mult, None, [("ps", b)], ["nac_tm"])
            self.ts(ncu_tm[:, blk, :], self.psb[b][:, 32:40], -1.0, None, ALU.mult, None, [("ps", b)], ["ncu_tm"])

        def bcast(dst, dres, src, nrow, h):
            for tg in range(NG):
                b = 6
                self.mm(self.psb[b][:, :], self.sel(h)[0:nrow, :], src[0:nrow, tg * 512:(tg + 1) * 512], True, True,
                        ["consts", "acf", "cuf"], [("ps", b)])
                self.act(dst[:, tg * 512:(tg + 1) * 512], self.psb[b][:, :], AF.Copy, [("ps", b)], [dres])

        def quad(Kd, dv, lhs_v, transform, den, finish, kres):
            for g in range(NG):
                bo = 4 if den else 4 + (g % 2)
                nkb = 4 * g + 4

                def issue_s(kb, g=g):
                    t0 = max(kb * P, g * 512)
                    n = (g + 1) * 512 - t0
                    diag = kb * P >= g * 512
                    bs = st["s"] % 4
                    st["s"] += 1
                    self.mm(self.psb[bs][:, 0:n], Kt[0:Kd, kb * P:(kb + 1) * P], Qt[0:Kd, t0:t0 + n], True, True,
                            ["Kt", "Qt"], [("ps", bs)])
                    return t0, n, diag, bs
                pend = [issue_s(kb) for kb in range(min(2, nkb))]
                for kb in range(nkb):
                    if kb + 2 < nkb:
                        pend.append(issue_s(kb + 2))
                    t0, n, diag, bs = pend[kb]
                    for vi, lv in enumerate(lhs_v):
                        pj = st["p"] % 3
                        st["p"] += 1
                        transform(vi, bs, kb, t0, n, diag, pj, g)
                        if diag:
                            self.tt(Pt[pj][:, 0:P], Pt[pj][:, 0:P], self.g_maskb[:, :], ALU.mult, [("Pt", pj), "maskb"],
                                    [("Pt", pj)], eng="pool")
                        first = (kb == 0 and vi == 0)
                        last = (kb == nkb - 1 and vi == len(lhs_v) - 1)
                        self.mm(self.psb[bo][0:dv, t0 - g * 512: 512], lv(kb), Pt[pj][:, 0:n], first, last,
                                [("Pt", pj)] + kres, [("ps", bo)])
                        if den:
                            self.mm(self.psb[5][:, t0 - g * 512: 512], self.g_onesb[:, :], Pt[pj][:, 0:n], first, last,
                                    [("Pt", pj), "onesb"], [("ps", 5)])
                finish(g, bo)

        scale_d = 128 ** -0.5
        for hd in range(4):
            self.dma("sp", [], ["Qt"], Qt[:, :], self.pd_q[hd * P:(hd + 1) * P, :])
            self.dma("sp", [], ["Kt"], Kt[:, :], self.pd_k[hd * P:(hd + 1) * P, :])
            self.dma("sp", [], ["Vt"], Vt[:, :, :], self.pd_v.rearrange("(b p) c -> p b c", p=P)[:, :, hd * P:(hd + 1) * P])
            bcast(bc, "bc", cuf, 8, hd)
            ys = yst[hd % 2]
            yres = ("yst", hd % 2)

            def tr_d(vi, bs, kb, t0, n, diag, pj, g, hd=hd):
                j = st["p"] % 2
                self.stt(tmpf[j][:, 0:n], self.psb[bs][:, 0:n], scale_d, bc[:, t0:t0 + n], ALU.mult, ALU.add,
                         [("ps", bs), "bc"], [("tmpf", j)])
                self.act(Pt[pj][:, 0:n], tmpf[j][:, 0:n], AF.Exp, [("tmpf", j), "ncu_tm"], [("Pt", pj)],
                         bias=ncu_tm[:, kb, hd:hd + 1])

            def fin_d(g, bo, ys=ys, yres=yres):
                self.op("dve", [("ps", 5)], ["rs"], lambda: nc.vector.reciprocal(rs[:, :], self.psb[5][:, :]))
                self.tt(ys[:, g * 512:(g + 1) * 512], self.psb[bo][:, :], rs[:, :], ALU.mult, [("ps", bo), "rs"], [yres])
            quad(P, P, [lambda kb: Vt[:, kb, :]], tr_d, True, fin_d, ["Vt"])
            ysend_rows(1024 + hd * P, ys, yres)

        for hd in range(4):
            self.dma("sp", [], ["Qt"], Qt[0:64, :], self.pc_q[hd * 64:(hd + 1) * 64, :])
            self.dma("sp", [], ["Kt"], Kt[0:64, :], self.pc_k[hd * 64:(hd + 1) * 64, :])
            self.dma("sp", [], ["Vt"], Vt[:, :, :], self.pc_v.rearrange("(b p) c -> p b c", p=P)[:, :, hd * P:(hd + 1) * P])
            self.dma("sp", [], ["Gt"], Gt[:, :], self.pc_g[hd * P:(hd + 1) * P, :])
            ys = yst[hd % 2]
            yres = ("yst", hd % 2)

            def tr_c(vi, bs, kb, t0, n, diag, pj, g, hd=hd):
                dd = dec[:, hd * 640: hd * 640 + P]
                dw = dec[:, hd * 640 + P: hd * 640 + 640]
                if diag:
                    self.tt(Pt[pj][:, 0:P], self.psb[bs][:, 0:P], dd, ALU.mult, [("ps", bs), "dec"], [("Pt", pj)])
                    if n > P:
                        self.tt(Pt[pj][:, P:n], self.psb[bs][:, P:n], dw[:, 0:n - P], ALU.mult,
                                [("ps", bs), "dec"], [("Pt", pj)])
                else:
                    m0 = 4 * g - kb
                    self.stt(Pt[pj][:, 0:n], self.psb[bs][:, 0:n], self.vec(l, "gpw", hd * 16 + m0), dw[:, 0:n],
                             ALU.mult, ALU.mult, [("ps", bs), "dec"] + vr, [("Pt", pj)])

            def fin_c(g, bo, ys=ys, yres=yres):
                gsl = slice(g * 512, (g + 1) * 512)
                self.act(of[:, :], self.psb[bo][:, :], AF.Copy, [("ps", bo)], ["of"])
                self.act(o2[:, :], self.psb[bo][:, :], AF.Square, [("ps", bo)], ["o2"])
                self.mm(self.psb[6][:, :], self.g_onesf[:, :], of[:, :], True, True, ["of", "onesf"], [("ps", 6)])
                self.mm(self.psb[7][:, :], self.g_onesf[:, :], o2[:, :], True, True, ["o2", "onesf"], [("ps", 7)])
                self.ts(mean[:, :], self.psb[6][:, :], 1.0 / P, None, ALU.mult, None, [("ps", 6)], ["mean"])
                self.tt(o2[:, :], mean[:, :], mean[:, :], ALU.mult, ["mean"], ["o2"])
                self.stt(rs[:, :], self.psb[7][:, :], 1.0 / P, o2[:, :], ALU.mult, ALU.subtract, [("ps", 7), "o2"], ["rs"])
                self.rsqrt(rs[:, :], rs[:, :], 1.0, EPS, ["rs"], ["rs"])
                self.tt(of[:, :], of[:, :], mean[:, :], ALU.subtract, ["of", "mean"], ["of"])
                self.tt(of[:, :], of[:, :], rs[:, :], ALU.mult, ["of", "rs"], ["of"])
                self.tt(ys[:, gsl], of[:, :], Gt[:, gsl], ALU.mult, ["of", "Gt"], [yres])
            quad(64, P, [lambda kb: Vt[:, kb, :]], tr_c, False, fin_c, ["Vt"])
            ysend_rows(512 + hd * P, ys, yres)

        self.barrier()
        self.sb_off = save
        xs1 = self.sb([P, L], BF16, "xs1")
        zf = self.sb([P, L], BF16, "zf")
        xse = self.sb([P, NB, 512], BF16, "xse")
        xso = self.sb([P, NB, 512], BF16, "xso")
        ygn = self.sb([P, 4, L], F32, "ygn")
        self.dma("sp", [], ["Kt"], Kt[:, :], self.pb_x[512:640, :])
        self.dma("sp", [], ["Qt"], Qt[:, :], self.pb_x[640:768, :])
        self.op("pool", [], ["xse"], lambda: nc.gpsimd.memset(xse[:, :, :], 0.0))
        self.op("pool", [], ["xso"], lambda: nc.gpsimd.memset(xso[:, :, :], 0.0))
        for ch in range(4):
            self.dma("sp", [], ["xs1"], xs1[:, :], self.pb_x[ch * P:(ch + 1) * P, :])
            for blk in range(NB):
                b = 6 + (blk % 2)
                pst = self.psb[b][:, :].bitcast(BF16)
                self.op("pe", ["xs1", "identb"], [("ps", b)],
                        lambda: nc.tensor.transpose(pst[:, 0:P], xs1[:, blk * P:(blk + 1) * P], self.g_identb[:, :]))
                self.op("act", [("ps", b)], ["xse"],
                        lambda: nc.scalar.copy(xse[:, blk, ch * P: ch * P + 64], pst[:, 0:64]))
                self.op("act", [("ps", b)], ["xso"],
                        lambda: nc.scalar.copy(xso[:, blk, ch * P + 64:(ch + 1) * P], pst[:, 64:P]))
        for pr in range(4):
            heads = [2 * pr, 2 * pr + 1]
            self.dma("sp", [], ["xs1"], xs1[:, :], self.pb_x[pr * P:(pr + 1) * P, :])
            self.dma("sp", [], ["zf"], zf[:, :], self.pb_z[pr * P:(pr + 1) * P, :])
            bcs = [bc, bc2]
            for vi, h in enumerate(heads):
                bcast(bcs[vi], ("bcs", vi), acf, 8, h)

            def tr_b(vi, bs, kb, t0, n, diag, pj, g, heads=heads, bcs=bcs):
                h = heads[vi]
                j = st["p"] % 2
                self.ts(tmpf[j][:, 0:n], bcs[vi][:, t0:t0 + n], nac_tm[:, kb, h:h + 1], 0.0, ALU.add, ALU.min,
                        [("bcs", vi), "nac_tm"], [("tmpf", j)], eng="pool")
                self.act(tmpe[j][:, 0:n], tmpf[j][:, 0:n], AF.Exp, [("tmpf", j)], [("tmpe", j)])
                self.stt(Pt[pj][:, 0:n], tmpe[j][:, 0:n], dt_tm[:, kb, h:h + 1], self.psb[bs][:, 0:n], ALU.mult, ALU.mult,
                         [("tmpe", j), "dt_tm", ("ps", bs)], [("Pt", pj)])

            def fin_b(g, bo, pr=pr):
                gsl = slice(g * 512, (g + 1) * 512)
                self.stt(of[:, :], xs1[:, gsl], self.vec(l, "dsk", pr), self.psb[bo][:, :], ALU.mult, ALU.add,
                         ["xs1", ("ps", bo)] + vr, ["of"])
                self.tt(ygn[:, pr, gsl], of[:, :], zf[:, gsl], ALU.mult, ["of", "zf"], ["ygn"])
            quad(P, P, [lambda kb, pr=pr: xse[:, kb, pr * P:(pr + 1) * P], lambda kb, pr=pr: xso[:, kb, pr * P:(pr + 1) * P]],
                 tr_b, False, fin_b, ["xse", "xso"])
        bstage = [yst[0], yst[1], Gt, Kt]
        bres = [("yst", 0), ("yst", 1), "Gt", "Kt"]
        for tg in range(NG):
            gsl = slice(tg * 512, (tg + 1) * 512)
            for ch in range(4):
                self.act(o2[:, :], ygn[:, ch, gsl], AF.Square, ["ygn"], ["o2"])
                self.mm(self.psb[7][:, :], self.g_onesf[:, :], o2[:, :], ch == 0, ch == 3, ["o2", "onesf"], [("ps", 7)])
            self.rsqrt(rs[:, :], self.psb[7][:, :], 1.0 / 512, EPS, [("ps", 7)], ["rs"])
            for ch in range(4):
                self.stt(bstage[ch][:, gsl], ygn[:, ch, gsl], self.vec(l, "ssm", ch), rs[:, :], ALU.mult, ALU.mult,
                         ["ygn", "rs"] + vr, [bres[ch]])
        for ch in range(4):
            ysend_rows(ch * P, bstage[ch], bres[ch])

        self.barrier()
        self.sb_off = save
        wT = self.sb([P, 8, P], F32, "wT")
        wTb = self.sb([P, 8, P], BF16, "wTb")
        bsb = self.sb([P, 8, P], F32, "bsb")
        va = [self.sb([P, 1024], BF16, "va%d" % i) for i in range(2)]
        ua = [self.sb([P, 8, P], BF16, "ua%d" % i) for i in range(2)]
        ya = self.sb([P, 8, LT], BF16, "ya")
        ta = self.sb([P, 512], F32, "ta")
        self.dma("sp", [], ["wT"], wT[:, :, :], self.i_sguwT[l])
        self.dma("sp", [], ["bsb"], bsb[:, :, :].rearrange("p g t -> p (g t)"), self.i_sgub[l, :, :].partition_broadcast(P))
        for g in range(8):
            self.tt(wTb[:, g, :], wT[:, g, :], self.cmask, ALU.mult, ["wT", "consts"], ["wTb"])
        for n in range(NBL):
            j = n % 2
            nsl = slice(n * P, (n + 1) * P)
            self.dma("sp", [], [("va", j)], va[j][:, :], self.pa_v[nsl, :])
            self.dma("sp", [], [("ua", j)], ua[j][:, :, :], self.pa_u.rearrange("(g p) t -> p g t", p=P)[:, :, nsl])
            for half in range(2):
                b = st["s"] % 4
                st["s"] += 1
                for gg in range(4):
                    g = half * 4 + gg
                    self.mm(self.psb[b][:, gg * P:(gg + 1) * P], va[j][:, g * P:(g + 1) * P], wTb[:, g, :], True, True,
                            [("va", j), "wTb"], [("ps", b)])
                self.tt(ta[:, :], self.psb[b][:, :], bsb[:, half * 4:(half + 1) * 4, :].rearrange("p g t -> p (g t)"), ALU.add,
                        [("ps", b), "bsb"], ["ta"])
                self.tt(ya[:, half * 4:(half + 1) * 4, nsl], ta[:, :].rearrange("p (g t) -> p g t", g=4),
                        ua[j][:, half * 4:(half + 1) * 4, :], ALU.mult, ["ta", ("ua", j)], ["ya"])
        self.dma("sp", ["ya"], [], self.ya_d.rearrange("(g p) t -> p g t", p=P), ya[:, :, :])
        self.allgather(self.ysend, self.yrecv, 512)

    def merge_phase(self, l):
        c = self.c
        KC, LT = c.KC, c.LT
        NTG = LT // 512
        self.phase_begin()
        vr = ["vecs%d" % l]
        hm = self.sb([P, KC, LT], BF16, "hm")
        yT = self.sb([P, 8, LT], BF16, "yT")
        ytmp = self.sb([P, 8, 512], BF16, "ytmp")
        acc = self.sb([P, KC, LT], F32, "acc")
        sg = [self.sb([P, 512], F32, "sg%d" % i) for i in range(2)]
        tm_ = [self.sb([P, 512], F32, "tm%d" % i) for i in range(2)]
        xt = [self.sb([P, 512], F32, "xt%d" % i) for i in range(2)]
        self.make_ring(3)
        wm = self.i_wsel[l]
        pi = 0
        CW = min(512, c.D)
        NCI = CW // P
        self.dma("sp", [], ["hm"], hm[:, :, :], self.hm_d.rearrange("(k p) t -> p k t", p=P))
        for n in range(4):
            if n == 0:
                self.dma("sp", [], ["yT"], yT[:, :, :], self.ya_d.rearrange("(k p) t -> p k t", p=P))
            else:
                for tg in range(NTG):
                    tsl = slice(tg * 512, (tg + 1) * 512)
                    for r in range(2):
                        row0 = ((0 * 3 + (n - 1)) * 2 + r) * 512
                        row1 = ((1 * 3 + (n - 1)) * 2 + r) * 512
                        self.dma("sp", [], ["yT"], yT[:, r * 4:(r + 1) * 4, tsl],
                                 self.yrecv[row0:row0 + 512, tsl].rearrange("(k p) t -> p k t", p=P))
                        self.dma("sp", [], ["ytmp"], ytmp[:, r * 4:(r + 1) * 4, :],
                                 self.yrecv[row1:row1 + 512, tsl].rearrange("(k p) t -> p k t", p=P))
                    self.ts(yT[:, :, tsl], yT[:, :, tsl], self.vec(l, "f0"), None, ALU.mult, None, ["yT"] + vr, ["yT"])
                    self.stt(yT[:, :, tsl], ytmp[:, :, :], self.vec(l, "f1"), yT[:, :, tsl], ALU.mult, ALU.add,
                             ["ytmp", "yT"] + vr, ["yT"])
            for dq in range(c.D // CW):
                rg, wg = self.load_w(wm[:, c.o_g + n * c.D + dq * CW: c.o_g + n * c.D + (dq + 1) * CW], KC, CW)
                rb, wb = self.load_w(self.i_wbr[l, n][:, dq * CW:(dq + 1) * CW], 8, CW)
                for tg in range(NTG):
                    tsl = slice(tg * 512, (tg + 1) * 512)
                    for dci in range(NCI):
                        dc = dq * NCI + dci
                        bg, bb = pi % 6, (pi + 1) % 6
                        pi += 2
                        for k in range(KC):
                            self.mm(self.psb[bg][:, :], wg[:, k, dci * P:(dci + 1) * P], hm[:, k, tsl], k == 0, k == KC - 1,
                                    [rg, "hm"], [("ps", bg)])
                        for k in range(8):
                            self.mm(self.psb[bb][:, :], wb[:, k, dci * P:(dci + 1) * P], yT[:, k, tsl], k == 0, k == 7,
                                    [rb, "yT"], [("ps", bb)])
                        j = (pi // 2) % 2
                        self.act(sg[j][:, :], self.psb[bg][:, :], AF.Sigmoid, [("ps", bg)], [("sg", j)])
                        if n == 0:
                            self.tt(acc[:, dc, tsl], sg[j][:, :], self.psb[bb][:, :], ALU.mult, [("sg", j), ("ps", bb)], ["acc"])
                        else:
                            self.tt(tm_[j][:, :], sg[j][:, :], self.psb[bb][:, :], ALU.mult, [("sg", j), ("ps", bb)], [("tm", j)])
                            self.tt(acc[:, dc, tsl], acc[:, dc, tsl], tm_[j][:, :], ALU.add, ["acc", ("tm", j)], ["acc"], eng="pool")
        for k in range(KC):
            self.act(hm[:, k, :], acc[:, k, :], AF.Copy, ["acc"], ["hm"])
        for dq in range(c.D // CW):
            ro, wo = self.load_w(self.i_wout[l][:, dq * CW:(dq + 1) * CW], KC, CW)
            for tg in range(NTG):
                tsl = slice(tg * 512, (tg + 1) * 512)
                for dci in range(NCI):
                    dc = dq * NCI + dci
                    b = pi % 6
                    pi += 1
                    j = pi % 2
                    self.dma("sp", [], [("xt", j)], xt[j][:, :], self.x_d[dc * P:(dc + 1) * P, tsl])
                    for k in range(KC):
                        self.mm(self.psb[b][:, :], wo[:, k, dci * P:(dci + 1) * P], hm[:, k, tsl], k == 0, k == KC - 1,
                                [ro, "hm"], [("ps", b)])
                    self.tt(xt[j][:, :], xt[j][:, :], self.psb[b][:, :], ALU.add, [("xt", j), ("ps", b)], [("xt", j)])
                    self.dma("sp", [("xt", j)], [], self.x_d[dc * P:(dc + 1) * P, tsl], xt[j][:, :])

    def build(self, stop_after=None):
        c = self.c
        self.declare_io()
        self.alloc_global()
        plan = []
        for l in range(c.depth):
            if l == 0:
                plan.append(("ffn", lambda: self.ffn_phase(0, 1, self.i_xT, True, "mix")))
            plan.append(("proj%d" % l, lambda l=l: self.proj_phase(l)))
            plan.append(("mix%d" % l, lambda l=l: self.mixer_phase(l)))
            plan.append(("merge%d" % l, lambda l=l: self.merge_phase(l)))
            if l < c.depth - 1:
                plan.append(("ffn2_%d" % l, lambda l=l: self.ffn_phase(l, 2, self.x_d, False, "ffn1next")))
            else:
                plan.append(("ffn2_%d" % l, lambda l=l: self.ffn_phase(l, 2, self.x_d, False, "final")))
        for name, fn in plan:
            fn()
            if stop_after == name:
                break
        self.barrier()
        return self.nc


def host_consts(cfg, j):
    L = cfg.L
    s = np.arange(P)[:, None]
    t = np.arange(P)[None, :]
    cmask = (s <= t).astype(np.float32)
    ident = np.eye(P, dtype=np.float32)
    sel = np.zeros((P, 16 * P), np.float32)
    for h in range(16):
        sel[h, h * P:(h + 1) * P] = 1.0
    consts = np.concatenate([cmask, ident, sel], axis=1)
    half = 32
    inv_freq = (1.0 / (np.float32(10000.0) ** (np.arange(half, dtype=np.float32) / half))).astype(np.float32)
    pos = np.arange(L, dtype=np.float32)
    ang = (pos[None, :] * inv_freq[:, None]).astype(np.float32)
    cos, sin = np.cos(ang).astype(np.float32), np.sin(ang).astype(np.float32)
    COS = np.concatenate([cos, cos, cos, cos], axis=0)
    SINS = np.concatenate([-sin, sin, -sin, sin], axis=0)
    rope = np.concatenate([COS, SINS], axis=1).astype(np.float32)
    dec = np.zeros((P, 4 * 640), np.float32)
    gpw = np.ones((4, 16), np.float32)
    ksc = 64 ** -0.5
    for hd in range(4):
        H = 4 * j + hd
        lg = np.log(np.float32(1.0) - np.float32(2.0) ** np.float32(-5.0 - H)).astype(np.float64)
        d = (t - s).astype(np.float64)
        dec[:, hd * 640: hd * 640 + P] = np.where(s <= t, np.exp(d * lg), 0.0) * ksc
        for qi in range(4):
            dec[:, hd * 640 + P + qi * P: hd * 640 + P + (qi + 1) * P] = np.exp((128.0 * (1 + qi) + d) * lg) * ksc
        for m in range(1, 16):
            gpw[hd, m] = np.exp(128.0 * (m - 1) * lg)
    return consts, rope, dec, gpw


def host_shared(cfg, inp):
    f = lambda a: np.ascontiguousarray(np.asarray(a, dtype=np.float32))
    return {
        "w1a": f(inp["ffn1_w_in"]), "w1b": f(inp["ffn1_w_out"]),
        "w2a": f(inp["ffn2_w_in"]), "w2b": f(inp["ffn2_w_out"]),
        "wbr": f(inp["w_branch"]), "wout": f(inp["w_mix_out"]),
        "lng": f(inp["sgu_norm"]).reshape(cfg.depth, 1, 1024),
        "sgub": f(inp["sgu_b"]).reshape(cfg.depth, 1, 1024),
        "sguwT": np.ascontiguousarray(np.transpose(f(inp["sgu_w"]), (0, 3, 1, 2))),
    }


def host_inputs(cfg, inp, b, j, shared):
    dep, KC, D, LT = cfg.depth, cfg.KC, cfg.D, cfg.LT
    consts, rope, dec, gpw = host_consts(cfg, j)
    wmix = np.asarray(inp["w_mix_in"], dtype=np.float32)
    ar = np.arange
    cq = 4624 + j * 256 + ar(256)
    ck = 5136 + j * 256 + ar(256)
    cols = np.concatenate([
        ar(0, 2048),
        2048 + j * 512 + ar(512),
        3072 + j * 512 + ar(512), 3072 + 1024 + j * 128 + ar(128), 3072 + 1280 + j * 128 + ar(128),
        4608 + j * 8 + ar(8),
        cq, ck, 5648 + j * 512 + ar(512), 6672 + j * 512 + ar(512),
        7696 + j * 512 + ar(512), 8720 + j * 512 + ar(512), 9744 + j * 512 + ar(512), 10768 + j * 4 + ar(4),
        10776 + ar(4 * D),
    ])
    assert cols.size == cfg.NSEL
    perm = np.concatenate([np.concatenate([ar(h * 64 + 32, h * 64 + 64), ar(h * 64, h * 64 + 32)]) for h in range(4)])
    conv_cols = np.concatenate([j * 512 + ar(512), 1024 + j * 128 + ar(128), 1280 + j * 128 + ar(128)])
    vecs = np.zeros((dep, P, cfg.NV), np.float32)
    pp = lambda v, n: np.asarray(v, np.float32).reshape(n, P).T
    for l in range(dep):
        o = 0
        vecs[l, :, o:o + KC] = pp(inp["ffn1_norm"][l], KC); o += KC
        vecs[l, :, o:o + KC] = pp(inp["mix_norm"][l], KC); o += KC
        vecs[l, :, o:o + KC] = pp(inp["ffn2_norm"][l], KC); o += KC
        cw = np.asarray(inp["conv_w"][l], np.float32)[:, conv_cols]
        for k in range(4):
            vecs[l, :, o:o + 6] = pp(cw[k], 6); o += 6
        vecs[l, :, o:o + 6] = pp(np.asarray(inp["conv_b"][l], np.float32)[conv_cols], 6); o += 6
        vecs[l, :, o:o + 4] = pp(np.repeat(np.asarray(inp["d_skip"][l], np.float32)[j * 8:(j + 1) * 8], 64), 4); o += 4
        vecs[l, :, o:o + 4] = pp(np.asarray(inp["ssm_norm"][l], np.float32)[j * 512:(j + 1) * 512], 4); o += 4
        vecs[l, 0:8, o] = np.asarray(inp["dt_bias"][l], np.float32)[j * 8:(j + 1) * 8]; o += 1
        vecs[l, 0:8, o] = np.asarray(inp["a_log"][l], np.float32)[j * 8:(j + 1) * 8]; o += 1
        vecs[l, 0:4, o] = np.asarray(inp["forget_bias"][l], np.float32)[j * 4:(j + 1) * 4]; o += 1
        vecs[l, :, o:o + KC] = pp(inp["final_norm"], KC); o += KC
        vecs[l, :, o:o + 64] = gpw.reshape(1, 64); o += 64
        vecs[l, :, o] = 1.0 if j == 0 else 0.0; o += 1
        vecs[l, :, o] = 1.0 if j == 1 else 0.0; o += 1
        assert o == cfg.NV
    d = dict(shared)
    d.update({
        "xT": np.ascontiguousarray(np.asarray(inp["x"], np.float32)[b].T[:, j * LT:(j + 1) * LT]),
        "wrot": np.ascontiguousarray(np.concatenate([wmix[:, :, cq[perm]], wmix[:, :, ck[perm]]], axis=2)),
        "vecs": vecs, "consts": consts, "rope": rope, "dec": dec,
    })
    for l in range(dep):
        d["wsel%d" % l] = np.ascontiguousarray(wmix[l][:, cols])
    return d


def kernel(**inputs):
    cfg = Cfg()
    x = np.asarray(inputs["x"])
    nb = x.shape[0]
    ncores = 2 * nb
    bld = Builder(cfg, ncores=ncores)
    nc = bld.build()
    shared = host_shared(cfg, inputs)
    in_maps = [host_inputs(cfg, inputs, c // 2, c % 2, shared) for c in range(ncores)]
    res = run_bass_kernel_spmd(nc, in_maps, core_ids=list(range(ncores)))
    out = np.empty((nb, cfg.L, cfg.D), np.float32)
    for c in range(ncores):
        b, j = c // 2, c % 2
        out[b, j * cfg.LT:(j + 1) * cfg.LT, :] = res.results[c]["outT"].T
    return out
```

```python
import math
import numpy as np
import concourse.bass as bass
import concourse.mybir as mybir
from concourse.bass_utils import run_bass_kernel_spmd

F32, BF16 = mybir.dt.float32, mybir.dt.bfloat16
AF = mybir.ActivationFunctionType
ALU = mybir.AluOpType
AX = mybir.AxisListType
P = 128
SB_BASE = 16512
SB_END = 229376
EPS = 1e-6
NCORES = 8


class Cfg:
    def __init__(self, L=2048, D=2048, DFF=5632, depth=2):
        self.L, self.D, self.DFF, self.depth = L, D, DFF, depth
        self.LT = L // 2
        self.KC = D // P
        self.NB = L // P
        self.NBL = self.LT // P
        self.TP = min(1024, self.LT)
        o = [0]

        def take(n):
            r = o[0]
            o[0] += n
            return r
        self.o_au, self.o_av = take(1024), take(1024)
        self.o_bz, self.o_bx, self.o_bdt = take(512), take(768), take(8)
        self.o_cq, self.o_ck, self.o_cv, self.o_cg = take(256), take(256), take(512), take(512)
        self.o_dq, self.o_dk, self.o_dv, self.o_df = take(512), take(512), take(512), take(4)
        self.o_g = take(4 * D)
        self.NSEL = o[0]
        self.NV = 4 * self.KC + 24 + 6 + 4 + 4 + 3 + 64 + 2


class Builder:
    def __init__(self, cfg, debug=False, ncores=8):
        self.c = cfg
        self.debug = debug
        self.ncores = ncores
        nc = bass.Bass("TRN2", target_bir_lowering=False)
        self.nc = nc
        self.eng = {"pe": nc.tensor, "act": nc.scalar, "dve": nc.vector, "pool": nc.gpsimd, "sp": nc.sync}
        self.esem = {k: nc.alloc_semaphore("es_" + k) for k in ["pe", "act", "dve", "pool"]}
        self.ecnt = {k: 0 for k in self.esem}
        self.seen = {k: {} for k in self.eng}
        self.lastw = {}
        self.readers = {}
        self.dsl = {q: [nc.alloc_semaphore("ds_%s%d" % (q, i)) for i in range(8)] for q in ["sp", "pool"]}
        self.dcnt = {q: [0] * 8 for q in self.dsl}
        self.dnext = {q: 0 for q in self.dsl}
        self.sb_off = SB_BASE
        self.uid = 0
        self.psb = [nc.alloc_psum_tensor("psb%d" % i, [P, 512], F32) for i in range(8)]

    def _wait(self, ek, ev):
        sem, val, _ = ev
        if self.seen[ek].get(sem.name, 0) < val:
            self.eng[ek].wait_ge(sem, val)
            self.seen[ek][sem.name] = val

    def _gather(self, ek, reads, writes, is_dma):
        deps = []
        for r in reads:
            w = self.lastw.get(r)
            if w is not None:
                deps.append(w)
        for w_ in writes:
            w = self.lastw.get(w_)
            if w is not None:
                deps.append(w)
            for rd in self.readers.get(w_, ()):
                if is_dma or rd[2] != ek:
                    deps.append(rd)
        for ev in deps:
            if (not is_dma) and ek == "pe" and ev[2] == "pe":
                continue
            self._wait(ek, ev)

    def _record(self, ev, reads, writes):
        for w_ in writes:
            self.lastw[w_] = ev
            self.readers[w_] = []
        for r in reads:
            lst = self.readers.setdefault(r, [])
            lst[:] = [x for x in lst if x[0].name != ev[0].name]
            lst.append(ev)

    def op(self, ek, reads, writes, fn):
        self._gather(ek, reads, writes, False)
        ins = fn()
        self.ecnt[ek] += 1
        ins.then_inc(self.esem[ek], 1)
        self._record((self.esem[ek], self.ecnt[ek], ek), reads, writes)

    def dma(self, q, reads, writes, out, in_):
        self._gather(q, reads, writes, True)
        i = self.dnext[q]
        self.dnext[q] = (i + 1) % 8
        sem = self.dsl[q][i]
        if self.dcnt[q][i] > 0:
            self._wait(q, (sem, 16 * self.dcnt[q][i], "dma"))
        if self.dcnt[q][i] >= 48:
            self.nsem_retired = getattr(self, "nsem_retired", 0) + 1
            sem = self.nc.alloc_semaphore("ds_%s%d_r%d" % (q, i, self.nsem_retired))
            self.dsl[q][i] = sem
            self.dcnt[q][i] = 0
        ins = self.eng[q].dma_start(out=out, in_=in_)
        self.dcnt[q][i] += 1
        ins.then_inc(sem, 16)
        self._record((sem, 16 * self.dcnt[q][i], "dma_" + q), reads, writes)

    def barrier(self):
        evs = [(self.esem[k], self.ecnt[k], k) for k in self.esem if self.ecnt[k] > 0]
        for q in self.dsl:
            for i in range(8):
                if self.dcnt[q][i] > 0:
                    evs.append((self.dsl[q][i], 16 * self.dcnt[q][i], "dma"))
        for ek in self.eng:
            for ev in evs:
                self._wait(ek, ev)
        self.lastw.clear()
        self.readers.clear()

    def sb(self, shape, dt, name):
        nbytes = int(np.prod(shape[1:])) * (4 if dt == F32 else 2)
        nbytes = (nbytes + 63) // 64 * 64
        assert self.sb_off + nbytes <= SB_END, ("SBUF overflow", name, self.sb_off, nbytes)
        self.uid += 1
        t = self.nc.alloc_sbuf_tensor_at("%s_%d" % (name, self.uid), list(shape), dt, offset=self.sb_off)
        self.sb_off += nbytes
        return t

    def dram(self, name, shape, dt):
        kind = "ExternalOutput" if self.debug else "Internal"
        return self.nc.dram_tensor(name, list(shape), dt, kind=kind).ap()

    def mm(self, out, lhsT, rhs, start, stop, reads, writes):
        self.op("pe", reads, writes, lambda: self.nc.tensor.matmul(out, lhsT, rhs, start=start, stop=stop))

    def act(self, out, in_, func, reads, writes, bias=None, scale=None):
        kw = {}
        if bias is not None:
            kw["bias"] = bias
        if scale is not None:
            kw["scale"] = scale
        self.op("act", reads, writes, lambda: self.nc.scalar.activation(out, in_, func, **kw))

    def rsqrt(self, out, in_, mult, add, reads, writes):
        self.act(out, in_, AF.Ln, reads, writes, bias=float(add), scale=float(mult))
        self.act(out, out, AF.Exp, writes, writes, scale=-0.5)

    def ts(self, out, in0, s1, s2, op0, op1, reads, writes, eng="dve"):
        e = self.nc.vector if eng == "dve" else self.nc.gpsimd
        if s2 is None:
            self.op(eng, reads, writes, lambda: e.tensor_scalar(out, in0, s1, None, op0))
        else:
            self.op(eng, reads, writes, lambda: e.tensor_scalar(out, in0, s1, s2, op0, op1))

    def stt(self, out, in0, s, in1, op0, op1, reads, writes, eng="dve"):
        e = self.nc.vector if eng == "dve" else self.nc.gpsimd
        self.op(eng, reads, writes, lambda: e.scalar_tensor_tensor(out, in0, s, in1, op0, op1))

    def tt(self, out, in0, in1, op, reads, writes, eng="dve"):
        e = self.nc.vector if eng == "dve" else self.nc.gpsimd
        self.op(eng, reads, writes, lambda: e.tensor_tensor(out, in0, in1, op))

    def declare_io(self):
        c, nc = self.c, self.nc
        dep = c.depth

        def inp(name, shape, dt=F32):
            return nc.dram_tensor(name, list(shape), dt, kind="ExternalInput").ap()

        def internal(name, shape, dt):
            return nc.dram_tensor(name, list(shape), dt, kind="Internal").ap()

        self.i_xT = inp("xT", [c.D, c.LT])
        self.i_w1a = inp("w1a", [dep, c.D, 2 * c.DFF])
        self.i_w1b = inp("w1b", [dep, c.DFF, c.D])
        self.i_w2a = inp("w2a", [dep, c.D, 2 * c.DFF])
        self.i_w2b = inp("w2b", [dep, c.DFF, c.D])
        self.i_wsel = [inp("wsel%d" % l, [c.D, c.NSEL]) for l in range(dep)]
        self.i_wrot = inp("wrot", [dep, c.D, 512])
        self.i_wbr = inp("wbr", [dep, 4, 1024, c.D])
        self.i_wout = inp("wout", [dep, c.D, c.D])
        self.i_vecs = inp("vecs", [dep, P, c.NV])
        self.i_lng = inp("lng", [dep, 1, 1024])
        self.i_sgub = inp("sgub", [dep, 1, 1024])
        self.i_sguwT = inp("sguwT", [dep, P, 8, P])
        self.i_consts = inp("consts", [P, 2 * P + 16 * P])
        self.i_rope = inp("rope", [P, 2 * c.L])
        self.i_dec = inp("dec", [P, 4 * 640])
        self.o_outT = nc.dram_tensor("outT", [c.D, c.LT], F32, kind="ExternalOutput").ap()
        self.x_d = self.dram("x_d", [c.D, c.LT], F32)
        self.hm_d = internal("hm_d", [c.D, c.LT], BF16)
        self.hm_g = internal("hm_g", [2 * c.D, c.LT], BF16)
        self.ya_d = self.dram("ya_d", [1024, c.LT], BF16)
        self.ysend = internal("ysend", [2 * 1536, c.LT], BF16)
        self.yrecv = internal("yrecv", [4 * 1536, c.LT], BF16)
        self.pa_u = self.dram("pa_u", [1024, c.LT], BF16)
        self.pa_v = self.dram("pa_v", [c.LT, 1024], BF16)
        self.pb_z = self.dram("pb_z", [512, c.L], BF16)
        self.pb_x = self.dram("pb_x", [768, c.L], BF16)
        self.pc_q = self.dram("pc_q", [256, c.L], BF16)
        self.pc_k = self.dram("pc_k", [256, c.L], BF16)
        self.pc_v = self.dram("pc_v", [c.L, 512], BF16)
        self.pc_g = self.dram("pc_g", [512, c.L], BF16)
        self.pd_q = self.dram("pd_q", [512, c.L], BF16)
        self.pd_k = self.dram("pd_k", [512, c.L], BF16)
        self.pd_v = self.dram("pd_v", [c.L, 512], BF16)

    def allgather(self, src, dst, rpp):
        self.barrier()
        self.ncc = getattr(self, "ncc", 0) + 1
        sem = self.nc.alloc_semaphore("cc%d" % self.ncc)
        groups = [[2 * i, 2 * i + 1] for i in range(self.ncores // 2)]
        npc = src.shape[0] // rpp
        for pc in range(npc):
            ins = self.nc.gpsimd.collective_compute("AllGather", ALU.bypass, replica_groups=groups,
                                                    ins=[src[pc * rpp:(pc + 1) * rpp, :].opt()],
                                                    outs=[dst[pc * 2 * rpp:(pc + 1) * 2 * rpp, :].opt()])
            ins.then_inc(sem)
            self.nc.gpsimd.wait_ge(sem, pc + 1)
        self.seen["pool"][sem.name] = npc
        for ek in self.eng:
            self._wait(ek, (sem, npc, "cc"))

    def alloc_global(self):
        c = self.c
        self.g_consts = self.sb([P, 2 * P + 16 * P], F32, "consts")
        self.g_vecs = [self.sb([P, c.NV], F32, "vecs%d" % l) for l in range(c.depth)]
        self.g_onesf = self.sb([P, P], F32, "onesf")
        self.g_onesb = self.sb([P, P], BF16, "onesb")
        self.g_identb = self.sb([P, P], BF16, "identb")
        self.g_maskb = self.sb([P, P], BF16, "maskb")
        self.g_dt = self.sb([P, c.L], F32, "dtraw")
        self.g_fl = self.sb([P, c.L], F32, "flraw")
        self.sb_phase = self.sb_off
        self.op("dve", [], ["flraw"], lambda: self.nc.vector.memset(self.g_fl[0:8, :], 0.0))
        self.dma("sp", [], ["consts"], self.g_consts[:, :], self.i_consts[:, :])
        for l in range(c.depth):
            self.dma("sp", [], ["vecs%d" % l], self.g_vecs[l][:, :], self.i_vecs[l, :, :])
        self.op("dve", [], ["onesf"], lambda: self.nc.vector.memset(self.g_onesf[:, :], 1.0))
        self.op("dve", [], ["onesb"], lambda: self.nc.vector.memset(self.g_onesb[:, :], 1.0))
        self.op("dve", ["consts"], ["identb"],
                lambda: self.nc.vector.tensor_copy(self.g_identb[:, :], self.g_consts[:, P:2 * P]))
        self.op("dve", ["consts"], ["maskb"],
                lambda: self.nc.vector.tensor_copy(self.g_maskb[:, :], self.g_consts[:, 0:P]))
        self.cmask = self.g_consts[:, 0:P]
        self.identf = self.g_consts[:, P:2 * P]

    def sel(self, h):
        return self.g_consts[0:16, 2 * P + h * P: 2 * P + (h + 1) * P]

    def vec(self, l, name, j=0):
        c = self.c
        k3 = 3 * c.KC
        offs = {"n1": 0, "nm": c.KC, "n2": 2 * c.KC, "cw": k3, "cb": k3 + 24, "dsk": k3 + 30, "ssm": k3 + 34,
                "dtb": k3 + 38, "alog": k3 + 39, "fb": k3 + 40, "nf": k3 + 41, "gpw": k3 + 41 + c.KC,
                "f0": k3 + 41 + c.KC + 64, "f1": k3 + 41 + c.KC + 65}
        o = offs[name] + j
        return self.g_vecs[l][:, o:o + 1]

    def phase_begin(self):
        self.barrier()
        self.epoch = getattr(self, "epoch", 0) + 1
        self.esem = {k: self.nc.alloc_semaphore("es%d_%s" % (self.epoch, k)) for k in ["pe", "act", "dve", "pool"]}
        self.ecnt = {k: 0 for k in self.esem}
        self.sb_off = self.sb_phase
        self.ring = None

    def make_ring(self, n):
        self.ring = [self.sb([P, 8192], BF16, "ring%d" % i) for i in range(n)]
        self.ring_i = 0

    def load_w(self, src, kc, ncol):
        i = self.ring_i
        self.ring_i = (i + 1) % len(self.ring)
        assert kc * ncol <= 8192
        view = self.ring[i][:, 0:kc * ncol].rearrange("p (k n) -> p k n", k=kc)
        self.dma("pool", [], [("ring", i)], view, src.rearrange("(k p) n -> p k n", p=P))
        return ("ring", i), view

    def rmsnorm(self, xT, xres, gname, l, hT, hres, T, tmp, out_f32=False):
        c = self.c
        sq, rstd = tmp
        for tg in range(T // 512):
            tsl = slice(tg * 512, (tg + 1) * 512)
            ps = self.psb[7]
            for k in range(c.KC):
                j = k % 2
                self.act(sq[j][:, :], xT[:, k, tsl], AF.Square, [xres], [("sq", j)])
                self.mm(ps[:, :], self.g_onesf[:, :], sq[j][:, :], k == 0, k == c.KC - 1,
                        [("sq", j), "onesf"], [("ps", 7)])
            self.rsqrt(rstd[:, :], ps[:, :], 1.0 / c.D, EPS, [("ps", 7)], ["rstd"])
            for k in range(c.KC):
                self.stt(hT[:, k, tsl], xT[:, k, tsl], self.vec(l, gname, k), rstd[:, :], ALU.mult, ALU.mult,
                         [xres, "rstd", "vecs%d" % l], [hres])

    def ffn(self, l, w_in, w_out, xT, hT, hid, sil, T):
        c = self.c
        G = 4
        ngr = c.DFF // (G * P)
        pi = 0
        for gi in range(ngr):
            rg, wg = self.load_w(w_in[:, gi * 512:(gi + 1) * 512], c.KC, 512)
            ru, wu = self.load_w(w_in[:, c.DFF + gi * 512: c.DFF + (gi + 1) * 512], c.KC, 512)
            hb = gi % 2
            for fc in range(G):
                for tg in range(T // 512):
                    tsl = slice(tg * 512, (tg + 1) * 512)
                    bg, bu = pi % 6, (pi + 1) % 6
                    pi += 2
                    for k in range(c.KC):
                        self.mm(self.psb[bg][:, :], wg[:, k, fc * P:(fc + 1) * P], hT[:, k, tsl], k == 0, k == c.KC - 1,
                                [rg, "hT"], [("ps", bg)])
                    for k in range(c.KC):
                        self.mm(self.psb[bu][:, :], wu[:, k, fc * P:(fc + 1) * P], hT[:, k, tsl], k == 0, k == c.KC - 1,
                                [ru, "hT"], [("ps", bu)])
                    sj = pi % 4 // 2
                    self.act(sil[sj][:, :], self.psb[bg][:, :], AF.Silu, [("ps", bg)], [("sil", sj)])
                    self.tt(hid[hb][:, fc, tsl], sil[sj][:, :], self.psb[bu][:, :], ALU.mult,
                            [("sil", sj), ("ps", bu)], [("hid", hb)])
            r2, w2 = self.load_w(w_out[gi * 512:(gi + 1) * 512, :], G, c.D)
            for dc in range(c.KC):
                for tg in range(T // 512):
                    tsl = slice(tg * 512, (tg + 1) * 512)
                    b = 6 + (pi % 2)
                    if b == 7:
                        b = 6
                    b = pi % 6
                    pi += 1
                    for g in range(G):
                        self.mm(self.psb[b][:, :], w2[:, g, dc * P:(dc + 1) * P], hid[hb][:, g, tsl], g == 0, g == G - 1,
                                [r2, ("hid", hb)], [("ps", b)])
                    self.stt(xT[:, dc, tsl], self.psb[b][:, :], 0.5, xT[:, dc, tsl], ALU.mult, ALU.add,
                             [("ps", b), "xT"], ["xT"])

    def ffn_phase(self, l, which, src_x, first_norm, post):
        c = self.c
        T = c.TP
        self.phase_begin()
        xT = self.sb([P, c.KC, T], F32, "xT")
        ovl0 = self.sb_off
        hT = self.sb([P, c.KC, T], BF16, "hT")
        hid = [self.sb([P, 4, T], BF16, "hid%d" % i) for i in range(2)]
        self.make_ring(3)
        ovl1 = self.sb_off
        sq = [self.sb([P, 512], F32, "sq%d" % i) for i in range(2)]
        rstd = self.sb([P, 512], F32, "rstd")
        sil = [self.sb([P, 512], F32, "sil%d" % i) for i in range(2)]
        save = self.sb_off
        self.sb_off = ovl0
        outF = self.sb([P, c.KC, T], F32, "outF") if post == "final" else None
        assert self.sb_off <= ovl1 or post != "final"
        self.sb_off = save
        w_in = (self.i_w1a if which == 1 else self.i_w2a)[l]
        w_out = (self.i_w1b if which == 1 else self.i_w2b)[l]
        xd = src_x.rearrange("(k p) t -> p k t", p=P)
        for ps_ in range(c.LT // T):
            tok = slice(ps_ * T, (ps_ + 1) * T)
            self.dma("sp", [], ["xT"], xT[:, :, :], xd[:, :, tok])
            self.rmsnorm(xT, "xT", "n1" if which == 1 else "n2", l, hT, "hT", T, (sq, rstd))
            self.ffn(l, w_in, w_out, xT, hT, hid, sil, T)
            if post == "ffn1next":
                self.rmsnorm(xT, "xT", "n1", l + 1, hT, "hT", T, (sq, rstd))
                self.ffn(l + 1, self.i_w1a[l + 1], self.i_w1b[l + 1], xT, hT, hid, sil, T)
                post_l = l + 1
            else:
                post_l = l
            if post in ("mix", "ffn1next"):
                self.rmsnorm(xT, "xT", "nm", post_l, hT, "hT", T, (sq, rstd))
                self.dma("sp", ["hT"], [], self.hm_d.rearrange("(k p) t -> p k t", p=P)[:, :, tok], hT[:, :, :])
                self.dma("sp", ["xT"], [], self.x_d.rearrange("(k p) t -> p k t", p=P)[:, :, tok], xT[:, :, :])
            else:
                self.barrier()
                self.rmsnorm(xT, "xT", "nf", l, outF, "outF", T, (sq, rstd))
                self.dma("sp", ["outF"], [], self.o_outT.rearrange("(k p) t -> p k t", p=P)[:, :, tok], outF[:, :, :])
                self.barrier()
        if post in ("mix", "ffn1next"):
            self.allgather(self.hm_d, self.hm_g, min(512, self.c.D))

    def proj_phase(self, l):
        c = self.c
        L, LT, KC = c.L, c.LT, c.KC
        self.phase_begin()
        hm = self.sb([P, KC, L], BF16, "hm")
        self.make_ring(3)
        lng = self.sb([P, 1024], F32, "lng")
        stg = [self.sb([P, L], BF16, "stg%d" % i) for i in range(2)]
        vrow = self.sb([P, 1024], F32, "vrow")
        vcen = self.sb([P, 1024], F32, "vcen")
        vsq = self.sb([P, 1024], F32, "vsq")
        vst = [self.sb([P, 1024], BF16, "vst%d" % i) for i in range(2)]
        sm = self.sb([P, 8], F32, "sm")
        t1 = [self.sb([P, 512], F32, "t1_%d" % i) for i in range(2)]
        t2 = [self.sb([P, 512], F32, "t2_%d" % i) for i in range(2)]
        ovl = self.sb_off
        xraw = self.sb([P, L + 4], F32, "xraw")
        cacc = self.sb([P, L], F32, "cacc")
        self.sb_off = ovl
        rope = self.sb([P, 2 * L], F32, "rope")
        self.dma("sp", [], ["lng"], lng[:, :], self.i_lng[l, :, :].partition_broadcast(P))
        self.op("dve", [], ["xraw"], lambda: self.nc.vector.memset(xraw[:, 0:4], 0.0))
        wm = self.i_wsel[l]
        st = {"pi": 0}

        def fm_section(col0, ncols, epi, ntok, M=P):
            nslot = (ncols + 511) // 512
            for s in range(nslot):
                n = min(512, ncols - s * 512)
                r, wv = self.load_w(wm[:, col0 + s * 512: col0 + s * 512 + n], KC, n)
                for cbi in range((n + M - 1) // M):
                    cb = s * 4 + cbi
                    for tg in range(ntok // 512):
                        b = st["pi"] % 6
                        st["pi"] += 1
                        for k in range(KC):
                            self.mm(self.psb[b][0:M, :], wv[:, k, cbi * M:(cbi + 1) * M], hm[:, k, tg * 512:(tg + 1) * 512],
                                    k == 0, k == KC - 1, [r, "hm"], [("ps", b)])
                        epi(cb, tg, b)

        def store_fm(dst, func, ntok):
            def epi(cb, tg, b):
                j = cb % 2
                self.act(stg[j][:, tg * 512:(tg + 1) * 512], self.psb[b][:, :], func, [("ps", b)], [("stg", j)])
                if tg == ntok // 512 - 1:
                    self.dma("sp", [("stg", j)], [], dst[cb * P:(cb + 1) * P, :], stg[j][:, 0:ntok])
            return epi

        def tm_section(col0, ncols, dst, ln, ntok):
            slots = [self.load_w(wm[:, col0 + h * 512: col0 + (h + 1) * 512], KC, 512) for h in range(ncols // 512)]
            for tb in range(ntok // P):
                bs = []
                for (r, wv) in slots:
                    b = st["pi"] % 6
                    st["pi"] += 1
                    bs.append(b)
                    for k in range(KC):
                        self.mm(self.psb[b][:, :], hm[:, k, tb * P:(tb + 1) * P], wv[:, k, :], k == 0, k == KC - 1,
                                [r, "hm"], [("ps", b)])
                j = tb % 2
                if not ln:
                    for half in range(len(slots)):
                        self.act(vst[j][:, half * 512:(half + 1) * 512], self.psb[bs[half]][:, :], AF.Copy,
                                 [("ps", bs[half])], [("vst", j)])
                else:
                    for half in range(2):
                        self.act(vrow[:, half * 512:(half + 1) * 512], self.psb[bs[half]][:, :], AF.Gelu_apprx_tanh,
                                 [("ps", bs[half])], ["vrow"])
                    self.op("dve", ["vrow"], ["sm0"], lambda: self.nc.vector.reduce_sum(sm[:, 0:1], vrow[:, :], axis=AX.X))
                    self.ts(sm[:, 1:2], sm[:, 0:1], -1.0 / 1024, None, ALU.mult, None, ["sm0"], ["sm1"])
                    self.ts(vcen[:, :], vrow[:, :], sm[:, 1:2], None, ALU.add, None, ["vrow", "sm1"], ["vcen"])
                    self.tt(vsq[:, :], vcen[:, :], vcen[:, :], ALU.mult, ["vcen"], ["vsq"])
                    self.op("dve", ["vsq"], ["sm2"], lambda: self.nc.vector.reduce_sum(sm[:, 2:3], vsq[:, :], axis=AX.X))
                    self.rsqrt(sm[:, 4:5], sm[:, 2:3], 1.0 / 1024, EPS, ["sm2"], ["sm4"])
                    self.stt(vst[j][:, :], vcen[:, :], sm[:, 4:5], lng[:, :], ALU.mult, ALU.mult,
                             ["vcen", "sm4", "lng"], [("vst", j)])
                self.dma("sp", [("vst", j)], [], dst[tb * P:(tb + 1) * P, :], vst[j][:, 0:ncols])

        self.dma("sp", [], ["hm"], hm[:, :, 0:LT], self.hm_d.rearrange("(k p) t -> p k t", p=P))
        fm_section(c.o_au, 1024, store_fm(self.pa_u, AF.Gelu_apprx_tanh, LT), LT)
        tm_section(c.o_av, 1024, self.pa_v, True, LT)
        self.barrier()
        rpp = min(512, c.D)
        kpp = rpp // P
        for pc in range(c.D // rpp):
            for r in range(2):
                row0 = (pc * 2 + r) * rpp
                self.dma("sp", [], ["hm"], hm[:, pc * kpp:(pc + 1) * kpp, r * LT:(r + 1) * LT],
                         self.hm_g[row0:row0 + rpp, :].rearrange("(k p) t -> p k t", p=P))
        fm_section(c.o_bz, 512, store_fm(self.pb_z, AF.Silu, L), L)

        def epi_xbc(cb, tg, b):
            j = cb % 2
            self.act(xraw[:, 4 + tg * 512: 4 + (tg + 1) * 512], self.psb[b][:, :], AF.Copy, [("ps", b)], ["xraw"])
            if tg == L // 512 - 1:
                vr = ["vecs%d" % l]
                self.ts(cacc[:, :], xraw[:, 1:1 + L], self.vec(l, "cw", cb), self.vec(l, "cb", cb),
                        ALU.mult, ALU.add, ["xraw"] + vr, ["cacc"])
                for kk in range(1, 4):
                    self.stt(cacc[:, :], xraw[:, 1 + kk:1 + kk + L], self.vec(l, "cw", kk * 6 + cb), cacc[:, :],
                             ALU.mult, ALU.add, ["xraw", "cacc"] + vr, ["cacc"])
                self.act(stg[j][:, :], cacc[:, :], AF.Silu, ["cacc"], [("stg", j)])
                self.dma("sp", [("stg", j)], [], self.pb_x[cb * P:(cb + 1) * P, :], stg[j][:, :])
        fm_section(c.o_bx, 768, epi_xbc, L)

        def epi_small(dst, res, M):
            def epi(cb, tg, b):
                self.act(dst[0:M, tg * 512:(tg + 1) * 512], self.psb[b][0:M, :], AF.Copy, [("ps", b)], [res])
            return epi
        fm_section(c.o_bdt, 8, epi_small(self.g_dt, "dtraw", 8), L, M=8)
        fm_section(c.o_df, 4, epi_small(self.g_fl, "flraw", 4), L, M=4)
        self.barrier()
        self.dma("sp", [], ["rope"], rope[:, :], self.i_rope[:, :])

        def rot_section(col0, rcol0, dst):
            r, wv = self.load_w(wm[:, col0: col0 + 256], KC, 256)
            r2, wv2 = self.load_w(self.i_wrot[l][:, rcol0: rcol0 + 256], KC, 256)
            for cb in range(2):
                j = cb % 2
                for tg in range(L // 512):
                    tsl = slice(tg * 512, (tg + 1) * 512)
                    b, b2 = st["pi"] % 6, (st["pi"] + 1) % 6
                    st["pi"] += 2
                    for k in range(KC):
                        self.mm(self.psb[b][:, :], wv[:, k, cb * P:(cb + 1) * P], hm[:, k, tsl], k == 0, k == KC - 1,
                                [r, "hm"], [("ps", b)])
                    for k in range(KC):
                        self.mm(self.psb[b2][:, :], wv2[:, k, cb * P:(cb + 1) * P], hm[:, k, tsl], k == 0, k == KC - 1,
                                [r2, "hm"], [("ps", b2)])
                    jj = tg % 2
                    self.tt(t1[jj][:, :], self.psb[b][:, :], rope[:, tsl], ALU.mult, [("ps", b), "rope"], [("t1", jj)])
                    self.tt(t2[jj][:, :], self.psb[b2][:, :], rope[:, L + tg * 512: L + (tg + 1) * 512], ALU.mult,
                            [("ps", b2), "rope"], [("t2", jj)])
                    self.tt(stg[j][:, tsl], t1[jj][:, :], t2[jj][:, :], ALU.add, [("t1", jj), ("t2", jj)], [("stg", j)],
                            eng="pool")
                self.dma("sp", [("stg", j)], [], dst[cb * P:(cb + 1) * P, :], stg[j][:, :])
        rot_section(c.o_cq, 0, self.pc_q)
        rot_section(c.o_ck, 256, self.pc_k)
        tm_section(c.o_cv, 512, self.pc_v, False, L)
        fm_section(c.o_cg, 512, store_fm(self.pc_g, AF.Silu, L), L)
        fm_section(c.o_dq, 512, store_fm(self.pd_q, AF.Copy, L), L)
        fm_section(c.o_dk, 512, store_fm(self.pd_k, AF.Copy, L), L)
        tm_section(c.o_dv, 512, self.pd_v, False, L)

    def mixer_phase(self, l):
        c = self.c
        L, NB, LT, NBL = c.L, c.NB, c.LT, c.NBL
        NG = L // 512
        self.phase_begin()
        nc = self.nc
        vr = ["vecs%d" % l]
        dec = self.sb([P, 4 * 640], F32, "dec")
        self.dma("sp", [], ["dec"], dec[:, :], self.i_dec[:, :])
        sm16 = self.sb([P, 8], F32, "sm16")
        acf = self.sb([P, L], F32, "acf")
        cuf = self.sb([P, L], F32, "cuf")
        dt_tm = self.sb([P, NB, 8], F32, "dt_tm")
        nac_tm = self.sb([P, NB, 8], F32, "nac_tm")
        ncu_tm = self.sb([P, NB, 8], F32, "ncu_tm")
        bc = self.sb([P, L], F32, "bc")
        bc2 = self.sb([P, L], F32, "bc2")
        Kt = self.sb([P, L], BF16, "Kt")
        Qt = self.sb([P, L], BF16, "Qt")
        Vt = self.sb([P, NB, P], BF16, "Vt")
        Gt = self.sb([P, L], BF16, "Gt")
        yst = [self.sb([P, L], BF16, "yst%d" % i) for i in range(2)]
        tmpf = [self.sb([P, 512], F32, "tmpf%d" % i) for i in range(2)]
        tmpe = [self.sb([P, 512], F32, "tmpe%d" % i) for i in range(2)]
        Pt = [self.sb([P, 512], BF16, "Pt%d" % i) for i in range(3)]
        of = self.sb([P, 512], F32, "of")
        o2 = self.sb([P, 512], F32, "o2")
        mean = self.sb([P, 512], F32, "mean")
        rs = self.sb([P, 512], F32, "rs")
        save = self.sb_off
        dtf = self.sb([P, L], F32, "dtf")
        dA = self.sb([P, L], F32, "dA")
        onesL = self.sb([P, L], F32, "onesL")
        st = {"p": 0, "s": 0}

        def ysend_rows(row0, src, src_res):
            for h in range(2):
                self.dma("sp", [src_res], [], self.ysend[h * 1536 + row0: h * 1536 + row0 + P, :], src[:, h * LT:(h + 1) * LT])

        self.op("dve", [], ["onesL"], lambda: nc.vector.memset(onesL[:, :], 1.0))
        self.act(sm16[0:8, 0:1], self.vec(l, "alog")[0:8, :], AF.Exp, vr, ["sm_a"])
        self.ts(sm16[0:8, 1:2], sm16[0:8, 0:1], -1.0, None, ALU.mult, None, ["sm_a"], ["sm_na"])
        self.act(dtf[0:8, :], self.g_dt[0:8, :], AF.Softplus, ["dtraw"] + vr, ["dtf"], bias=self.vec(l, "dtb")[0:8, :])
        self.ts(dA[0:8, :], dtf[0:8, :], sm16[0:8, 1:2], None, ALU.mult, None, ["dtf", "sm_na"], ["dA"])
        self.op("dve", ["dA", "onesL"], ["acf"],
                lambda: nc.vector.tensor_tensor_scan(acf[0:8, :], onesL[0:8, :], dA[0:8, :], 0.0, ALU.mult, ALU.add))
        self.ts(sm16[0:8, 2:3], self.vec(l, "fb")[0:8, :], -1.0, None, ALU.mult, None, vr, ["sm_nfb"])
        self.act(dA[0:8, :], self.g_fl[0:8, :], AF.Softplus, ["flraw", "sm_nfb", "acf"], ["dA"], bias=sm16[0:8, 2:3], scale=-1.0)
        self.ts(dA[0:8, :], dA[0:8, :], -1.0, None, ALU.mult, None, ["dA"], ["dA"])
        self.op("dve", ["dA", "onesL"], ["cuf"],
                lambda: nc.vector.tensor_tensor_scan(cuf[0:8, :], onesL[0:8, :], dA[0:8, :], 0.0, ALU.mult, ALU.add))
        for blk in range(NB):
            bsl = slice(blk * P, (blk + 1) * P)
            b = 6
            self.op("pe", ["dtf", "consts"], [("ps", b)],
                    lambda: nc.tensor.transpose(self.psb[b][:, 0:8], dtf[0:8, bsl], self.identf[0:8, 0:8]))
            self.op("pe", ["acf", "consts"], [("ps", b)],
                    lambda: nc.tensor.transpose(self.psb[b][:, 16:24], acf[0:8, bsl], self.identf[0:8, 0:8]))
            self.op("pe", ["cuf", "consts"], [("ps", b)],
                    lambda: nc.tensor.transpose(self.psb[b][:, 32:40], cuf[0:8, bsl], self.identf[0:8, 0:8]))
            self.ts(dt_tm[:, blk, :], self.psb[b][:, 0:8], 1.0, None, ALU.mult, None, [("ps", b)], ["dt_tm"])
            self.ts(nac_tm[:, blk, :], self.psb[b][:, 16:24], -1.0, None, ALU.mult, None, [("ps", b)], ["nac_tm"])
            self.ts(ncu_tm[:, blk, :], self.psb[b][:, 32:40], -1.0, None, ALU.mult, None, [("ps", b)], ["ncu_tm"])

        def bcast(dst, dres, src, nrow, h):
            for tg in range(NG):
                b = 6
                self.mm(self.psb[b][:, :], self.sel(h)[0:nrow, :], src[0:nrow, tg * 512:(tg + 1) * 512], True, True,
                        ["consts", "acf", "cuf"], [("ps", b)])
                self.act(dst[:, tg * 512:(tg + 1) * 512], self.psb[b][:, :], AF.Copy, [("ps", b)], [dres])

        def quad(Kd, dv, lhs_v, transform, den, finish, kres, need_mask=True):
            for g in range(NG):
                bo = 4 if den else 4 + (g % 2)
                nkb = 4 * g + 4

                def issue_s(kb, g=g):
                    t0 = max(kb * P, g * 512)
                    n = (g + 1) * 512 - t0
                    diag = kb * P >= g * 512
                    bs = st["s"] % 4
                    st["s"] += 1
                    self.mm(self.psb[bs][:, 0:n], Kt[0:Kd, kb * P:(kb + 1) * P], Qt[0:Kd, t0:t0 + n], True, True,
                            ["Kt", "Qt"], [("ps", bs)])
                    return t0, n, diag, bs
                pend = [issue_s(kb) for kb in range(min(2, nkb))]
                for kb in range(nkb):
                    if kb + 2 < nkb:
                        pend.append(issue_s(kb + 2))
                    t0, n, diag, bs = pend[kb]
                    for vi, lv in enumerate(lhs_v):
                        pj = st["p"] % 3
                        st["p"] += 1
                        transform(vi, bs, kb, t0, n, diag, pj, g)
                        if diag and need_mask:
                            self.tt(Pt[pj][:, 0:P], Pt[pj][:, 0:P], self.g_maskb[:, :], ALU.mult, [("Pt", pj), "maskb"],
                                    [("Pt", pj)], eng="pool")
                        first = (kb == 0 and vi == 0)
                        last = (kb == nkb - 1 and vi == len(lhs_v) - 1)
                        self.mm(self.psb[bo][0:dv, t0 - g * 512: 512], lv(kb), Pt[pj][:, 0:n], first, last,
                                [("Pt", pj)] + kres, [("ps", bo)])
                        if den:
                            self.mm(self.psb[5][:, t0 - g * 512: 512], self.g_onesb[:, :], Pt[pj][:, 0:n], first, last,
                                    [("Pt", pj), "onesb"], [("ps", 5)])
                finish(g, bo)

        scale_d = 128 ** -0.5
        for hd in range(4):
            self.dma("sp", [], ["Qt"], Qt[:, :], self.pd_q[hd * P:(hd + 1) * P, :])
            self.dma("sp", [], ["Kt"], Kt[:, :], self.pd_k[hd * P:(hd + 1) * P, :])
            self.dma("sp", [], ["Vt"], Vt[:, :, :], self.pd_v.rearrange("(b p) c -> p b c", p=P)[:, :, hd * P:(hd + 1) * P])
            bcast(bc, "bc", cuf, 8, hd)
            ys = yst[hd % 2]
            yres = ("yst", hd % 2)

            def tr_d(vi, bs, kb, t0, n, diag, pj, g, hd=hd):
                j = st["p"] % 2
                self.stt(tmpf[j][:, 0:n], self.psb[bs][:, 0:n], scale_d, bc[:, t0:t0 + n], ALU.mult, ALU.add,
                         [("ps", bs), "bc"], [("tmpf", j)])
                self.act(Pt[pj][:, 0:n], tmpf[j][:, 0:n], AF.Exp, [("tmpf", j), "ncu_tm"], [("Pt", pj)],
                         bias=ncu_tm[:, kb, hd:hd + 1])

            def fin_d(g, bo, ys=ys, yres=yres):
                self.op("dve", [("ps", 5)], ["rs"], lambda: nc.vector.reciprocal(rs[:, :], self.psb[5][:, :]))
                self.tt(ys[:, g * 512:(g + 1) * 512], self.psb[bo][:, :], rs[:, :], ALU.mult, [("ps", bo), "rs"], [yres])
            quad(P, P, [lambda kb: Vt[:, kb, :]], tr_d, True, fin_d, ["Vt"])
            ysend_rows(1024 + hd * P, ys, yres)

        for hd in range(4):
            self.dma("sp", [], ["Qt"], Qt[0:64, :], self.pc_q[hd * 64:(hd + 1) * 64, :])
            self.dma("sp", [], ["Kt"], Kt[0:64, :], self.pc_k[hd * 64:(hd + 1) * 64, :])
            self.dma("sp", [], ["Vt"], Vt[:, :, :], self.pc_v.rearrange("(b p) c -> p b c", p=P)[:, :, hd * P:(hd + 1) * P])
            self.dma("sp", [], ["Gt"], Gt[:, :], self.pc_g[hd * P:(hd + 1) * P, :])
            ys = yst[hd % 2]
            yres = ("yst", hd % 2)

            def tr_c(vi, bs, kb, t0, n, diag, pj, g, hd=hd):
                dd = dec[:, hd * 640: hd * 640 + P]
                dw = dec[:, hd * 640 + P: hd * 640 + 640]
                if diag:
                    self.tt(Pt[pj][:, 0:P], self.psb[bs][:, 0:P], dd, ALU.mult, [("ps", bs), "dec"], [("Pt", pj)])
                    if n > P:
                        self.tt(Pt[pj][:, P:n], self.psb[bs][:, P:n], dw[:, 0:n - P], ALU.mult,
                                [("ps", bs), "dec"], [("Pt", pj)])
                else:
                    m0 = 4 * g - kb
                    self.stt(Pt[pj][:, 0:n], self.psb[bs][:, 0:n], self.vec(l, "gpw", hd * 16 + m0), dw[:, 0:n],
                             ALU.mult, ALU.mult, [("ps", bs), "dec"] + vr, [("Pt", pj)])

            def fin_c(g, bo, ys=ys, yres=yres):
                gsl = slice(g * 512, (g + 1) * 512)
                self.act(of[:, :], self.psb[bo][:, :], AF.Copy, [("ps", bo)], ["of"])
                self.act(o2[:, :], self.psb[bo][:, :], AF.Square, [("ps", bo)], ["o2"])
                self.mm(self.psb[6][:, :], self.g_onesf[:, :], of[:, :], True, True, ["of", "onesf"], [("ps", 6)])
                self.mm(self.psb[7][:, :], self.g_onesf[:, :], o2[:, :], True, True, ["o2", "onesf"], [("ps", 7)])
                self.ts(mean[:, :], self.psb[6][:, :], 1.0 / P, None, ALU.mult, None, [("ps", 6)], ["mean"])
                self.tt(o2[:, :], mean[:, :], mean[:, :], ALU.mult, ["mean"], ["o2"])
                self.stt(rs[:, :], self.psb[7][:, :], 1.0 / P, o2[:, :], ALU.mult, ALU.subtract, [("ps", 7), "o2"], ["rs"])
                self.rsqrt(rs[:, :], rs[:, :], 1.0, EPS, ["rs"], ["rs"])
                self.tt(of[:, :], of[:, :], mean[:, :], ALU.subtract, ["of", "mean"], ["of"])
                self.tt(of[:, :], of[:, :], rs[:, :], ALU.mult, ["of", "rs"], ["of"])
                self.tt(ys[:, gsl], of[:, :], Gt[:, gsl], ALU.mult, ["of", "Gt"], [yres])
            quad(64, P, [lambda kb: Vt[:, kb, :]], tr_c, False, fin_c, ["Vt"], need_mask=False)
            ysend_rows(512 + hd * P, ys, yres)

        self.barrier()
        self.sb_off = save
        xs1 = self.sb([P, L], BF16, "xs1")
        zf = self.sb([P, L], BF16, "zf")
        xse = self.sb([P, NB, 512], BF16, "xse")
        xso = self.sb([P, NB, 512], BF16, "xso")
        ygn = self.sb([P, 4, L], F32, "ygn")
        self.dma("sp", [], ["Kt"], Kt[:, :], self.pb_x[512:640, :])
        self.dma("sp", [], ["Qt"], Qt[:, :], self.pb_x[640:768, :])
        self.op("pool", [], ["xse"], lambda: nc.gpsimd.memset(xse[:, :, :], 0.0))
        self.op("pool", [], ["xso"], lambda: nc.gpsimd.memset(xso[:, :, :], 0.0))
        for ch in range(4):
            self.dma("sp", [], ["xs1"], xs1[:, :], self.pb_x[ch * P:(ch + 1) * P, :])
            for blk in range(NB):
                b = 6 + (blk % 2)
                pst = self.psb[b][:, :].bitcast(BF16)
                self.op("pe", ["xs1", "identb"], [("ps", b)],
                        lambda: nc.tensor.transpose(pst[:, 0:P], xs1[:, blk * P:(blk + 1) * P], self.g_identb[:, :]))
                self.op("act", [("ps", b)], ["xse"],
                        lambda: nc.scalar.copy(xse[:, blk, ch * P: ch * P + 64], pst[:, 0:64]))
                self.op("act", [("ps", b)], ["xso"],
                        lambda: nc.scalar.copy(xso[:, blk, ch * P + 64:(ch + 1) * P], pst[:, 64:P]))
        for pr in range(4):
            heads = [2 * pr, 2 * pr + 1]
            self.dma("sp", [], ["xs1"], xs1[:, :], self.pb_x[pr * P:(pr + 1) * P, :])
            self.dma("sp", [], ["zf"], zf[:, :], self.pb_z[pr * P:(pr + 1) * P, :])
            bcs = [bc, bc2]
            for vi, h in enumerate(heads):
                bcast(bcs[vi], ("bcs", vi), acf, 8, h)

            def tr_b(vi, bs, kb, t0, n, diag, pj, g, heads=heads, bcs=bcs):
                h = heads[vi]
                j = st["p"] % 2
                nb_ = nac_tm[:, kb, h:h + 1]
                if diag:
                    self.ts(tmpf[j][:, 0:P], bcs[vi][:, t0:t0 + P], nb_, 0.0, ALU.add, ALU.min,
                            [("bcs", vi), "nac_tm"], [("tmpf", j)])
                    self.act(tmpe[j][:, 0:P], tmpf[j][:, 0:P], AF.Exp, [("tmpf", j)], [("tmpe", j)])
                    if n > P:
                        self.act(tmpe[j][:, P:n], bcs[vi][:, t0 + P:t0 + n], AF.Exp, [("bcs", vi), "nac_tm"], [("tmpe", j)],
                                 bias=nb_)
                else:
                    self.act(tmpe[j][:, 0:n], bcs[vi][:, t0:t0 + n], AF.Exp, [("bcs", vi), "nac_tm"], [("tmpe", j)], bias=nb_)
                self.stt(Pt[pj][:, 0:n], tmpe[j][:, 0:n], dt_tm[:, kb, h:h + 1], self.psb[bs][:, 0:n], ALU.mult, ALU.mult,
                         [("tmpe", j), "dt_tm", ("ps", bs)], [("Pt", pj)])

            def fin_b(g, bo, pr=pr):
                gsl = slice(g * 512, (g + 1) * 512)
                self.stt(of[:, :], xs1[:, gsl], self.vec(l, "dsk", pr), self.psb[bo][:, :], ALU.mult, ALU.add,
                         ["xs1", ("ps", bo)] + vr, ["of"])
                self.tt(ygn[:, pr, gsl], of[:, :], zf[:, gsl], ALU.mult, ["of", "zf"], ["ygn"])
            quad(P, P, [lambda kb, pr=pr: xse[:, kb, pr * P:(pr + 1) * P], lambda kb, pr=pr: xso[:, kb, pr * P:(pr + 1) * P]],
                 tr_b, False, fin_b, ["xse", "xso"])
        bstage = [yst[0], yst[1], Gt, Kt]
        bres = [("yst", 0), ("yst", 1), "Gt", "Kt"]
        for tg in range(NG):
            gsl = slice(tg * 512, (tg + 1) * 512)
            for ch in range(4):
                self.act(o2[:, :], ygn[:, ch, gsl], AF.Square, ["ygn"], ["o2"])
                self.mm(self.psb[7][:, :], self.g_onesf[:, :], o2[:, :], ch == 0, ch == 3, ["o2", "onesf"], [("ps", 7)])
            self.rsqrt(rs[:, :], self.psb[7][:, :], 1.0 / 512, EPS, [("ps", 7)], ["rs"])
            for ch in range(4):
                self.stt(bstage[ch][:, gsl], ygn[:, ch, gsl], self.vec(l, "ssm", ch), rs[:, :], ALU.mult, ALU.mult,
                         ["ygn", "rs"] + vr, [bres[ch]])
        for ch in range(4):
            ysend_rows(ch * P, bstage[ch], bres[ch])

        self.barrier()
        self.sb_off = save
        wT = self.sb([P, 8, P], F32, "wT")
        wTb = self.sb([P, 8, P], BF16, "wTb")
        bsb = self.sb([P, 8, P], F32, "bsb")
        va = [self.sb([P, 1024], BF16, "va%d" % i) for i in range(2)]
        ua = [self.sb([P, 8, P], BF16, "ua%d" % i) for i in range(2)]
        ya = self.sb([P, 8, LT], BF16, "ya")
        ta = self.sb([P, 512], F32, "ta")
        self.dma("sp", [], ["wT"], wT[:, :, :], self.i_sguwT[l])
        self.dma("sp", [], ["bsb"], bsb[:, :, :].rearrange("p g t -> p (g t)"), self.i_sgub[l, :, :].partition_broadcast(P))
        for g in range(8):
            self.tt(wTb[:, g, :], wT[:, g, :], self.cmask, ALU.mult, ["wT", "consts"], ["wTb"])
        for n in range(NBL):
            j = n % 2
            nsl = slice(n * P, (n + 1) * P)
            self.dma("sp", [], [("va", j)], va[j][:, :], self.pa_v[nsl, :])
            self.dma("sp", [], [("ua", j)], ua[j][:, :, :], self.pa_u.rearrange("(g p) t -> p g t", p=P)[:, :, nsl])
            for half in range(2):
                b = st["s"] % 4
                st["s"] += 1
                for gg in range(4):
                    g = half * 4 + gg
                    self.mm(self.psb[b][:, gg * P:(gg + 1) * P], va[j][:, g * P:(g + 1) * P], wTb[:, g, :], True, True,
                            [("va", j), "wTb"], [("ps", b)])
                self.tt(ta[:, :], self.psb[b][:, :], bsb[:, half * 4:(half + 1) * 4, :].rearrange("p g t -> p (g t)"), ALU.add,
                        [("ps", b), "bsb"], ["ta"])
                self.tt(ya[:, half * 4:(half + 1) * 4, nsl], ta[:, :].rearrange("p (g t) -> p g t", g=4),
                        ua[j][:, half * 4:(half + 1) * 4, :], ALU.mult, ["ta", ("ua", j)], ["ya"])
        self.dma("sp", ["ya"], [], self.ya_d.rearrange("(g p) t -> p g t", p=P), ya[:, :, :])
        self.allgather(self.ysend, self.yrecv, 512)

    def merge_phase(self, l):
        c = self.c
        KC, LT = c.KC, c.LT
        NTG = LT // 512
        self.phase_begin()
        vr = ["vecs%d" % l]
        hm = self.sb([P, KC, LT], BF16, "hm")
        yT = self.sb([P, 8, LT], BF16, "yT")
        ytmp = self.sb([P, 8, 512], BF16, "ytmp")
        acc = self.sb([P, KC, LT], F32, "acc")
        sg = [self.sb([P, 512], F32, "sg%d" % i) for i in range(2)]
        tm_ = [self.sb([P, 512], F32, "tm%d" % i) for i in range(2)]
        xt = [self.sb([P, 512], F32, "xt%d" % i) for i in range(2)]
        self.make_ring(3)
        wm = self.i_wsel[l]
        pi = 0
        CW = min(512, c.D)
        NCI = CW // P
        self.dma("sp", [], ["hm"], hm[:, :, :], self.hm_d.rearrange("(k p) t -> p k t", p=P))
        for n in range(4):
            if n == 0:
                self.dma("sp", [], ["yT"], yT[:, :, :], self.ya_d.rearrange("(k p) t -> p k t", p=P))
            else:
                for tg in range(NTG):
                    tsl = slice(tg * 512, (tg + 1) * 512)
                    for r in range(2):
                        row0 = ((0 * 3 + (n - 1)) * 2 + r) * 512
                        row1 = ((1 * 3 + (n - 1)) * 2 + r) * 512
                        self.dma("sp", [], ["yT"], yT[:, r * 4:(r + 1) * 4, tsl],
                                 self.yrecv[row0:row0 + 512, tsl].rearrange("(k p) t -> p k t", p=P))
                        self.dma("sp", [], ["ytmp"], ytmp[:, r * 4:(r + 1) * 4, :],
                                 self.yrecv[row1:row1 + 512, tsl].rearrange("(k p) t -> p k t", p=P))
                    self.ts(yT[:, :, tsl], yT[:, :, tsl], self.vec(l, "f0"), None, ALU.mult, None, ["yT"] + vr, ["yT"])
                    self.stt(yT[:, :, tsl], ytmp[:, :, :], self.vec(l, "f1"), yT[:, :, tsl], ALU.mult, ALU.add,
                             ["ytmp", "yT"] + vr, ["yT"])
            for dq in range(c.D // CW):
                rg, wg = self.load_w(wm[:, c.o_g + n * c.D + dq * CW: c.o_g + n * c.D + (dq + 1) * CW], KC, CW)
                rb, wb = self.load_w(self.i_wbr[l, n][:, dq * CW:(dq + 1) * CW], 8, CW)
                for tg in range(NTG):
                    tsl = slice(tg * 512, (tg + 1) * 512)
                    for dci in range(NCI):
                        dc = dq * NCI + dci
                        bg, bb = pi % 6, (pi + 1) % 6
                        pi += 2
                        for k in range(KC):
                            self.mm(self.psb[bg][:, :], wg[:, k, dci * P:(dci + 1) * P], hm[:, k, tsl], k == 0, k == KC - 1,
                                    [rg, "hm"], [("ps", bg)])
                        for k in range(8):
                            self.mm(self.psb[bb][:, :], wb[:, k, dci * P:(dci + 1) * P], yT[:, k, tsl], k == 0, k == 7,
                                    [rb, "yT"], [("ps", bb)])
                        j = (pi // 2) % 2
                        self.act(sg[j][:, :], self.psb[bg][:, :], AF.Sigmoid, [("ps", bg)], [("sg", j)])
                        if n == 0:
                            self.tt(acc[:, dc, tsl], sg[j][:, :], self.psb[bb][:, :], ALU.mult, [("sg", j), ("ps", bb)], ["acc"])
                        else:
                            self.tt(tm_[j][:, :], sg[j][:, :], self.psb[bb][:, :], ALU.mult, [("sg", j), ("ps", bb)], [("tm", j)])
                            self.tt(acc[:, dc, tsl], acc[:, dc, tsl], tm_[j][:, :], ALU.add, ["acc", ("tm", j)], ["acc"], eng="pool")
        for k in range(KC):
            self.act(hm[:, k, :], acc[:, k, :], AF.Copy, ["acc"], ["hm"])
        for dq in range(c.D // CW):
            ro, wo = self.load_w(self.i_wout[l][:, dq * CW:(dq + 1) * CW], KC, CW)
            for tg in range(NTG):
                tsl = slice(tg * 512, (tg + 1) * 512)
                for dci in range(NCI):
                    dc = dq * NCI + dci
                    b = pi % 6
                    pi += 1
                    j = pi % 2
                    self.dma("sp", [], [("xt", j)], xt[j][:, :], self.x_d[dc * P:(dc + 1) * P, tsl])
                    for k in range(KC):
                        self.mm(self.psb[b][:, :], wo[:, k, dci * P:(dci + 1) * P], hm[:, k, tsl], k == 0, k == KC - 1,
                                [ro, "hm"], [("ps", b)])
                    self.tt(xt[j][:, :], xt[j][:, :], self.psb[b][:, :], ALU.add, [("xt", j), ("ps", b)], [("xt", j)])
                    self.dma("sp", [("xt", j)], [], self.x_d[dc * P:(dc + 1) * P, tsl], xt[j][:, :])

    def build(self, stop_after=None):
        c = self.c
        self.declare_io()
        self.alloc_global()
        plan = []
        for l in range(c.depth):
            if l == 0:
                plan.append(("ffn", lambda: self.ffn_phase(0, 1, self.i_xT, True, "mix")))
            plan.append(("proj%d" % l, lambda l=l: self.proj_phase(l)))
            plan.append(("mix%d" % l, lambda l=l: self.mixer_phase(l)))
            plan.append(("merge%d" % l, lambda l=l: self.merge_phase(l)))
            if l < c.depth - 1:
                plan.append(("ffn2_%d" % l, lambda l=l: self.ffn_phase(l, 2, self.x_d, False, "ffn1next")))
            else:
                plan.append(("ffn2_%d" % l, lambda l=l: self.ffn_phase(l, 2, self.x_d, False, "final")))
        for name, fn in plan:
            fn()
            if stop_after == name:
                break
        self.barrier()
        return self.nc


def host_consts(cfg, j):
    L = cfg.L
    s = np.arange(P)[:, None]
    t = np.arange(P)[None, :]
    cmask = (s <= t).astype(np.float32)
    ident = np.eye(P, dtype=np.float32)
    sel = np.zeros((P, 16 * P), np.float32)
    for h in range(16):
        sel[h, h * P:(h + 1) * P] = 1.0
    consts = np.concatenate([cmask, ident, sel], axis=1)
    half = 32
    inv_freq = (1.0 / (np.float32(10000.0) ** (np.arange(half, dtype=np.float32) / half))).astype(np.float32)
    pos = np.arange(L, dtype=np.float32)
    ang = (pos[None, :] * inv_freq[:, None]).astype(np.float32)
    cos, sin = np.cos(ang).astype(np.float32), np.sin(ang).astype(np.float32)
    COS = np.concatenate([cos, cos, cos, cos], axis=0)
    SINS = np.concatenate([-sin, sin, -sin, sin], axis=0)
    rope = np.concatenate([COS, SINS], axis=1).astype(np.float32)
    dec = np.zeros((P, 4 * 640), np.float32)
    gpw = np.ones((4, 16), np.float32)
    ksc = 64 ** -0.5
    for hd in range(4):
        H = 4 * j + hd
        lg = np.log(np.float32(1.0) - np.float32(2.0) ** np.float32(-5.0 - H)).astype(np.float64)
        d = (t - s).astype(np.float64)
        dec[:, hd * 640: hd * 640 + P] = np.where(s <= t, np.exp(d * lg), 0.0) * ksc
        for qi in range(4):
            dec[:, hd * 640 + P + qi * P: hd * 640 + P + (qi + 1) * P] = np.exp((128.0 * (1 + qi) + d) * lg) * ksc
        for m in range(1, 16):
            gpw[hd, m] = np.exp(128.0 * (m - 1) * lg)
    return consts, rope, dec, gpw


def host_shared(cfg, inp):
    f = lambda a: np.ascontiguousarray(np.asarray(a, dtype=np.float32))
    return {
        "w1a": f(inp["ffn1_w_in"]), "w1b": f(inp["ffn1_w_out"]),
        "w2a": f(inp["ffn2_w_in"]), "w2b": f(inp["ffn2_w_out"]),
        "wbr": f(inp["w_branch"]), "wout": f(inp["w_mix_out"]),
        "lng": f(inp["sgu_norm"]).reshape(cfg.depth, 1, 1024),
        "sgub": f(inp["sgu_b"]).reshape(cfg.depth, 1, 1024),
        "sguwT": np.ascontiguousarray(np.transpose(f(inp["sgu_w"]), (0, 3, 1, 2))),
    }


def host_inputs(cfg, inp, b, j, shared):
    dep, KC, D, LT = cfg.depth, cfg.KC, cfg.D, cfg.LT
    consts, rope, dec, gpw = host_consts(cfg, j)
    wmix = np.asarray(inp["w_mix_in"], dtype=np.float32)
    ar = np.arange
    cq = 4624 + j * 256 + ar(256)
    ck = 5136 + j * 256 + ar(256)
    cols = np.concatenate([
        ar(0, 2048),
        2048 + j * 512 + ar(512),
        3072 + j * 512 + ar(512), 3072 + 1024 + j * 128 + ar(128), 3072 + 1280 + j * 128 + ar(128),
        4608 + j * 8 + ar(8),
        cq, ck, 5648 + j * 512 + ar(512), 6672 + j * 512 + ar(512),
        7696 + j * 512 + ar(512), 8720 + j * 512 + ar(512), 9744 + j * 512 + ar(512), 10768 + j * 4 + ar(4),
        10776 + ar(4 * D),
    ])
    assert cols.size == cfg.NSEL
    perm = np.concatenate([np.concatenate([ar(h * 64 + 32, h * 64 + 64), ar(h * 64, h * 64 + 32)]) for h in range(4)])
    conv_cols = np.concatenate([j * 512 + ar(512), 1024 + j * 128 + ar(128), 1280 + j * 128 + ar(128)])
    vecs = np.zeros((dep, P, cfg.NV), np.float32)
    pp = lambda v, n: np.asarray(v, np.float32).reshape(n, P).T
    for l in range(dep):
        o = 0
        vecs[l, :, o:o + KC] = pp(inp["ffn1_norm"][l], KC); o += KC
        vecs[l, :, o:o + KC] = pp(inp["mix_norm"][l], KC); o += KC
        vecs[l, :, o:o + KC] = pp(inp["ffn2_norm"][l], KC); o += KC
        cw = np.asarray(inp["conv_w"][l], np.float32)[:, conv_cols]
        for k in range(4):
            vecs[l, :, o:o + 6] = pp(cw[k], 6); o += 6
        vecs[l, :, o:o + 6] = pp(np.asarray(inp["conv_b"][l], np.float32)[conv_cols], 6); o += 6
        vecs[l, :, o:o + 4] = pp(np.repeat(np.asarray(inp["d_skip"][l], np.float32)[j * 8:(j + 1) * 8], 64), 4); o += 4
        vecs[l, :, o:o + 4] = pp(np.asarray(inp["ssm_norm"][l], np.float32)[j * 512:(j + 1) * 512], 4); o += 4
        vecs[l, 0:8, o] = np.asarray(inp["dt_bias"][l], np.float32)[j * 8:(j + 1) * 8]; o += 1
        vecs[l, 0:8, o] = np.asarray(inp["a_log"][l], np.float32)[j * 8:(j + 1) * 8]; o += 1
        vecs[l, 0:4, o] = np.asarray(inp["forget_bias"][l], np.float32)[j * 4:(j + 1) * 4]; o += 1
        vecs[l, :, o:o + KC] = pp(inp["final_norm"], KC); o += KC
        vecs[l, :, o:o + 64] = gpw.reshape(1, 64); o += 64
        vecs[l, :, o] = 1.0 if j == 0 else 0.0; o += 1
        vecs[l, :, o] = 1.0 if j == 1 else 0.0; o += 1
        assert o == cfg.NV
    d = dict(shared)
    d.update({
        "xT": np.ascontiguousarray(np.asarray(inp["x"], np.float32)[b].T[:, j * LT:(j + 1) * LT]),
        "wrot": np.ascontiguousarray(np.concatenate([wmix[:, :, cq[perm]], wmix[:, :, ck[perm]]], axis=2)),
        "vecs": vecs, "consts": consts, "rope": rope, "dec": dec,
    })
    for l in range(dep):
        d["wsel%d" % l] = np.ascontiguousarray(wmix[l][:, cols])
    return d


def kernel(**inputs):
    cfg = Cfg()
    x = np.asarray(inputs["x"])
    nb = x.shape[0]
    ncores = 2 * nb
    bld = Builder(cfg, ncores=ncores)
    nc = bld.build()
    shared = host_shared(cfg, inputs)
    in_maps = [host_inputs(cfg, inputs, c // 2, c % 2, shared) for c in range(ncores)]
    res = run_bass_kernel_spmd(nc, in_maps, core_ids=list(range(ncores)))
    out = np.empty((nb, cfg.L, cfg.D), np.float32)
    for c in range(ncores):
        b, j = c // 2, c % 2
        out[b, j * cfg.LT:(j + 1) * cfg.LT, :] = res.results[c]["outT"].T
    return out
```
